# Optimizing a Trainium2 kernel written in Bass

```python
import math
import jax, jax.numpy as jnp
from jax import lax
import numpy as np

D_MODEL = 1024
BATCH = 2
SEQ = 8192
DEPTH = 4

N_EVEN = (DEPTH + 1) // 2
N_ODD = DEPTH // 2
D_FF = 2816
RMS_EPS = 1e-6
LN_EPS = 1e-5
SC_WIDTH = D_MODEL // 2
SC_KERNEL = 3
CM_WIDTH = D_MODEL // 2
CM_KERNEL = 31
IN_AB = 3 * SC_WIDTH + 2 * CM_WIDTH
N_HEADS = 16
HEAD_DIM = D_MODEL // N_HEADS
ATTN_WIDTH = N_HEADS * HEAD_DIM
DILATED_BRANCHES = ((128, 1), (512, 4), (2048, 16))
ATTN_BLOCK = 128
NEG_INF = -1e30

kernel_name = "hybrid_shortconv_conformer_dilated_macaron"


def rms_norm(x, g):
    xf = x.astype(jnp.float32)
    y = xf * lax.rsqrt(jnp.mean(xf * xf, axis=-1, keepdims=True) + RMS_EPS)
    return (y * g.astype(jnp.float32)).astype(x.dtype)


def layer_norm(x, g, b):
    xf = x.astype(jnp.float32)
    mu = jnp.mean(xf, axis=-1, keepdims=True)
    xc = xf - mu
    y = xc * lax.rsqrt(jnp.mean(xc * xc, axis=-1, keepdims=True) + LN_EPS)
    return (y * g.astype(jnp.float32) + b.astype(jnp.float32)).astype(x.dtype)


def swiglu_ffn(x, w_gate_up, w_down):
    gate, up = jnp.split(x @ w_gate_up, 2, axis=-1)
    return (jax.nn.silu(gate) * up) @ w_down


def causal_depthwise_conv(x, w):
    k_len, ch = w.shape
    return lax.conv_general_dilated(
        x, w[:, None, :].astype(x.dtype), window_strides=(1,), padding=[(k_len - 1, 0)],
        dimension_numbers=('NWC', 'WIO', 'NWC'), feature_group_count=ch)


def conv_mixers(h, w_in, a_kernel, b_kernel, b_bias, b_ln_g, b_ln_b, w_out):
    z = h @ w_in
    a_b, a_c, a_x, b_val, b_gate = jnp.split(
        z, [SC_WIDTH, 2 * SC_WIDTH, 3 * SC_WIDTH, 3 * SC_WIDTH + CM_WIDTH], axis=-1)
    y_a = a_b * causal_depthwise_conv(a_c * a_x, a_kernel)
    u = b_val * jax.nn.sigmoid(b_gate)
    u = causal_depthwise_conv(u, b_kernel) + b_bias.astype(u.dtype)
    y_b = jax.nn.silu(layer_norm(u, b_ln_g, b_ln_b))
    return jnp.concatenate([y_a, y_b], axis=-1) @ w_out


def alibi_slopes(n_heads):
    return np.array([2.0 ** (-8.0 * (i + 1) / n_heads) for i in range(n_heads)], dtype=np.float32)


def dilated_branch(q, k, v, slopes, window, dil):
    bsz, seq, n_h, d_h = q.shape
    span = window // dil
    assert span <= ATTN_BLOCK
    n_sub = seq // dil
    n_pad = -(-n_sub // ATTN_BLOCK) * ATTN_BLOCK
    n_blk = n_pad // ATTN_BLOCK

    def to_blocks(t):
        t = t.reshape(bsz, n_sub, dil, n_h, d_h).transpose(0, 2, 1, 3, 4)
        t = jnp.pad(t, ((0, 0), (0, 0), (0, n_pad - n_sub), (0, 0), (0, 0)))
        return t.reshape(bsz, dil, n_blk, ATTN_BLOCK, n_h, d_h)

    def with_prev(t):
        prev = jnp.pad(t, ((0, 0), (0, 0), (1, 0), (0, 0), (0, 0), (0, 0)))[:, :, :-1]
        return jnp.concatenate([prev, t], axis=3)

    qb = to_blocks(q.astype(jnp.float32)) * (1.0 / math.sqrt(d_h))
    kk = with_prev(to_blocks(k.astype(jnp.float32)))
    vv = with_prev(to_blocks(v.astype(jnp.float32)))

    scores = jnp.einsum('brnqhd,brnkhd->brnhqk', qb, kk)
    q_pos = jnp.arange(ATTN_BLOCK) + ATTN_BLOCK
    k_pos = jnp.arange(2 * ATTN_BLOCK)
    rel = q_pos[:, None] - k_pos[None, :]
    key_abs = jnp.arange(n_blk)[:, None] * ATTN_BLOCK - ATTN_BLOCK + k_pos[None, :]
    valid = ((rel >= 0) & (rel <= span))[None] & (key_abs >= 0)[:, None, :]
    bias = -slopes[:, None, None] * (dil * rel).astype(jnp.float32)[None]
    scores = jnp.where(valid[None, None, :, None], scores + bias, NEG_INF)
    lse = jax.nn.logsumexp(scores, axis=-1)
    p = jnp.exp(scores - lse[..., None])
    out = jnp.einsum('brnhqk,brnkhd->brnqhd', p, vv)

    out = out.reshape(bsz, dil, n_pad, n_h, d_h)[:, :, :n_sub]
    out = out.transpose(0, 2, 1, 3, 4).reshape(bsz, seq, n_h, d_h)
    lse = lse.transpose(0, 1, 2, 4, 3).reshape(bsz, dil, n_pad, n_h)[:, :, :n_sub]
    lse = lse.transpose(0, 2, 1, 3).reshape(bsz, seq, n_h)
    return out, lse


def dilated_attention(h, w_qkv, w_o):
    bsz, seq, _ = h.shape
    qkv = (h @ w_qkv).reshape(bsz, seq, 3, N_HEADS, HEAD_DIM)
    q, k, v = qkv[:, :, 0], qkv[:, :, 1], qkv[:, :, 2]
    slopes = jnp.asarray(alibi_slopes(N_HEADS))
    branches = [dilated_branch(q, k, v, slopes, w, d) for (w, d) in DILATED_BRANCHES]
    outs = jnp.stack([o for o, _ in branches], axis=0)
    lses = jnp.stack([l for _, l in branches], axis=0)
    alpha = jax.nn.softmax(lses, axis=0)
    o = jnp.einsum('gbsh,gbshd->bshd', alpha, outs)
    return o.reshape(bsz, seq, ATTN_WIDTH).astype(h.dtype) @ w_o


def setup_inputs(seed: int = 0) -> dict:
    key = jax.random.key(seed)
    ks = jax.random.split(key, 24)
    f32 = jnp.float32

    def dense(k, shape, fan_in):
        return jax.random.normal(k, shape, f32) * (fan_in ** -0.5)

    def gain(k, shape):
        return 1.0 + 0.01 * jax.random.normal(k, shape, f32)

    def small(k, shape):
        return 0.02 * jax.random.normal(k, shape, f32)

    return {
        'x': jax.random.normal(ks[0], (BATCH, SEQ, D_MODEL), f32),
        'ffn1_norm': gain(ks[1], (DEPTH, D_MODEL)),
        'ffn1_w_gate_up': dense(ks[2], (DEPTH, D_MODEL, 2 * D_FF), D_MODEL),
        'ffn1_w_down': dense(ks[3], (DEPTH, D_FF, D_MODEL), D_FF),
        'mix_norm': gain(ks[4], (DEPTH, D_MODEL)),
        'ffn2_norm': gain(ks[5], (DEPTH, D_MODEL)),
        'ffn2_w_gate_up': dense(ks[6], (DEPTH, D_MODEL, 2 * D_FF), D_MODEL),
        'ffn2_w_down': dense(ks[7], (DEPTH, D_FF, D_MODEL), D_FF),
        'conv_w_in': dense(ks[8], (N_EVEN, D_MODEL, IN_AB), D_MODEL),
        'conv_a_kernel': dense(ks[9], (N_EVEN, SC_KERNEL, SC_WIDTH), SC_KERNEL),
        'conv_b_kernel': dense(ks[10], (N_EVEN, CM_KERNEL, CM_WIDTH), CM_KERNEL),
        'conv_b_bias': small(ks[11], (N_EVEN, CM_WIDTH)),
        'conv_b_ln_gain': gain(ks[12], (N_EVEN, CM_WIDTH)),
        'conv_b_ln_bias': small(ks[13], (N_EVEN, CM_WIDTH)),
        'conv_w_out': dense(ks[14], (N_EVEN, SC_WIDTH + CM_WIDTH, D_MODEL), SC_WIDTH + CM_WIDTH),
        'attn_w_qkv': dense(ks[15], (N_ODD, D_MODEL, 3 * ATTN_WIDTH), D_MODEL),
        'attn_w_o': dense(ks[16], (N_ODD, ATTN_WIDTH, D_MODEL), ATTN_WIDTH),
        'final_norm': gain(ks[17], (D_MODEL,)),
    }


def reference(x, ffn1_norm, ffn1_w_gate_up, ffn1_w_down, mix_norm, ffn2_norm, ffn2_w_gate_up,
              ffn2_w_down, conv_w_in, conv_a_kernel, conv_b_kernel, conv_b_bias, conv_b_ln_gain,
              conv_b_ln_bias, conv_w_out, attn_w_qkv, attn_w_o, final_norm):
    for layer in range(DEPTH):
        x = x + 0.5 * swiglu_ffn(rms_norm(x, ffn1_norm[layer]), ffn1_w_gate_up[layer], ffn1_w_down[layer])
        h = rms_norm(x, mix_norm[layer])
        i = layer // 2
        if layer % 2 == 0:
            x = x + conv_mixers(h, conv_w_in[i], conv_a_kernel[i], conv_b_kernel[i], conv_b_bias[i],
                                conv_b_ln_gain[i], conv_b_ln_bias[i], conv_w_out[i])
        else:
            x = x + dilated_attention(h, attn_w_qkv[i], attn_w_o[i])
        x = x + 0.5 * swiglu_ffn(rms_norm(x, ffn2_norm[layer]), ffn2_w_gate_up[layer], ffn2_w_down[layer])
    return rms_norm(x, final_norm)
```

```python
import numpy as np
import ml_dtypes
import concourse.bass as bass
import concourse.mybir as mybir
from concourse.bass_utils import run_bass_kernel_spmd

F32 = mybir.dt.float32
BF16 = mybir.dt.bfloat16
ALU = mybir.AluOpType
AF = mybir.ActivationFunctionType

NCORES = 8
T = 2048
D = 1024
KC = 8
DFF = 2816
NJ = 22
TB = 512
NTB = T // TB
DEPTH = 4
RMS_EPS = 1e-6
LN_EPS = 1e-5
FF_GROUPS = [6, 6, 5, 5]


class Sem:
    def __init__(self, nc, name):
        self.h = nc.alloc_semaphore(name)
        self.count = 0
        self.name = name


class Buf:
    __slots__ = ("name", "w", "r")

    def __init__(self, name):
        self.name = name
        self.w = None
        self.r = []


class Eng:
    def __init__(self, nc, name, attr):
        self.name = name
        self.attr = attr
        self.sem = Sem(nc, "s_" + name)
        self.ops = []
        self.seen = {}

    def need(self, dep):
        if dep is None:
            return
        sem, val = dep
        if sem is self.sem and (self.name == "pe" or val > sem.count):
            return
        if self.seen.get(sem, 0) >= val:
            return
        self.seen[sem] = val
        h = sem.h
        self.ops.append(lambda e, h=h, val=val: e.wait_ge(h, val))


class Prog:
    def __init__(self, nc):
        self.nc = nc
        self.pe = Eng(nc, "pe", "tensor")
        self.act = Eng(nc, "act", "scalar")
        self.dve = Eng(nc, "dve", "vector")
        self.pool = Eng(nc, "pool", "gpsimd")
        self.sp = Eng(nc, "sp", "sync")
        self.engs = [self.pe, self.act, self.dve, self.pool, self.sp]
        self.dsems = []

    def dma_sem(self, name):
        s = Sem(self.nc, name)
        self.dsems.append(s)
        return s

    def op(self, eng, fn, reads=(), writes=(), inc=True):
        for b in reads:
            eng.need(b.w)
        for b in writes:
            eng.need(b.w)
            for d in b.r:
                eng.need(d)
        val = eng.sem.count + 1
        if inc:
            eng.sem.count = val
            h = eng.sem.h
            eng.ops.append(lambda e, fn=fn, h=h: fn(e).then_inc(h, 1))
        else:
            eng.ops.append(lambda e, fn=fn: fn(e))
        me = (eng.sem, val)
        for b in reads:
            b.r = [d for d in b.r if d[0] is not eng.sem] + [me]
        for b in writes:
            b.w = me
            b.r = []

    def dma(self, eng, fn, dsem, reads=(), writes=(), final=None):
        for b in reads:
            eng.need(b.w)
        for b in writes:
            eng.need(b.w)
            for d in b.r:
                eng.need(d)
        dsem.count += 16
        h = dsem.h
        eng.ops.append(lambda e, fn=fn, h=h: fn(e).then_inc(h, 16))
        me = (dsem, final if final is not None else dsem.count)
        for b in reads:
            b.r = b.r + [me]
        for b in writes:
            b.w = me
            b.r = []

    def coll(self, fn, csem, reads=(), writes=()):
        eng = self.pool
        for b in reads:
            eng.need(b.w)
        for b in writes:
            eng.need(b.w)
            for d in b.r:
                eng.need(d)
        csem.count += 1
        h = csem.h
        eng.ops.append(lambda e, fn=fn, h=h: fn(e).then_inc(h, 1))
        me = (csem, csem.count)
        for b in reads:
            b.r = b.r + [me]
        for b in writes:
            b.w = me
            b.r = []

    def barrier(self, engs=None):
        engs = engs or [self.pe, self.act, self.dve]
        for e in engs:
            for f in engs:
                if f is not e and f.sem.count > 0:
                    e.need((f.sem, f.sem.count))

    def finish(self, final_deps):
        for d in final_deps:
            self.sp.need(d)
        nc = self.nc
        with nc.Block() as block:
            for eng in self.engs:
                def body(e, eng=eng):
                    for f in eng.ops:
                        f(e)
                getattr(block, eng.attr)(body)


class MK:
    def __init__(self, phases, single=False):
        self.phases = phases
        self.single = single
        nc = bass.Bass("TRN2", target_bir_lowering=False)
        self.nc = nc
        self.P = Prog(nc)
        P = self.P
        self.xT_in = nc.dram_tensor("xT", [D, T], F32, kind="ExternalInput").ap()
        self.gains_in = nc.dram_tensor("gains", [128, 13 * KC], F32, kind="ExternalInput").ap()
        self.wgu_in = nc.dram_tensor("wgu", [8 * NJ, 128, KC * 256], F32, kind="ExternalInput").ap()
        self.wdn_in = nc.dram_tensor("wdn", [8 * NJ, 128, D], F32, kind="ExternalInput").ap()
        self.out_ap = nc.dram_tensor("out", [D, T], F32, kind="ExternalOutput").ap()
        self.wcv_in = nc.dram_tensor("wcv", [8, 128, KC * 640], F32, kind="ExternalInput").ap()
        self.wco_in = nc.dram_tensor("wco", [16, 128, D], F32, kind="ExternalInput").ap()
        self.cvp_in = nc.dram_tensor("cvp", [128, 8 * 37], F32, kind="ExternalInput").ap()
        self.wqkv_in = nc.dram_tensor("wqkv", [16, 128, KC * 384], F32, kind="ExternalInput").ap()
        self.wo_in = nc.dram_tensor("wo", [16, 128, D], F32, kind="ExternalInput").ap()
        self.tab_in = nc.dram_tensor("tab", [16, 128, 17 * 128], F32, kind="ExternalInput").ap()
        self.sel_in = nc.dram_tensor("sel", [128, 9], F32, kind="ExternalInput").ap()
        self.xch = {}
        for tag, ncols in (("c0", 32), ("c1", 32), ("a0", T), ("a1", T)):
            xin = nc.dram_tensor("xin_" + tag, [128, KC * ncols], BF16)
            xag = nc.dram_tensor("xag_" + tag, [NCORES * 128, KC * ncols], BF16)
            self.xch[tag] = (xin, xag, Buf("xin" + tag), Buf("xag" + tag))
        self.x = nc.alloc_sbuf_tensor("x", [128, KC, T], F32)
        self.xb = [[Buf(f"x{kc}_{tb}") for tb in range(NTB)] for kc in range(KC)]
        self.hT = nc.alloc_sbuf_tensor("hT", [128, KC, T], BF16)
        self.hb = [[Buf(f"h{kc}_{tb}") for tb in range(NTB)] for kc in range(KC)]
        self.gains = nc.alloc_sbuf_tensor("gains_sb", [128, 13 * KC], F32)
        self.gains_b = Buf("gains")
        self.ones = nc.alloc_sbuf_tensor("ones", [128, 128], BF16)
        self.ones_b = Buf("ones")
        self.eps_rms = nc.alloc_sbuf_tensor("eps_rms", [128, 1], F32)
        self.eps_ln = nc.alloc_sbuf_tensor("eps_ln", [128, 1], F32)
        GMAX = max(FF_GROUPS)
        self.AR = 45056
        self.arena = nc.alloc_sbuf_tensor("arena", [128, self.AR], BF16)
        ar = self.arena
        self.actT = ar[:, 0:GMAX * T].rearrange("p (g t) -> p g t", g=GMAX)
        self.actb = [[Buf(f"a{j}_{tb}") for tb in range(NTB)] for j in range(GMAX)]
        self.NDN = 12
        o = GMAX * T
        self.wdn = ar[:, o:o + self.NDN * D].rearrange("p (s d) -> p s d", s=self.NDN)
        self.wdnb = [Buf(f"wdn{i}") for i in range(self.NDN)]
        self.wdns = [P.dma_sem(f"d_wdn{i}") for i in range(self.NDN)]
        self.wdn_i = 0
        o += self.NDN * D
        self.NGU = 4
        self.wgu = ar[:, o:o + self.NGU * KC * 256].rearrange("p (s d) -> p s d", s=self.NGU)
        self.wgub = [Buf(f"wgu{i}") for i in range(self.NGU)]
        self.wgus = [P.dma_sem(f"d_wgu{i}") for i in range(self.NGU)]
        self.wgu_i = 0
        self.sq = nc.alloc_sbuf_tensor("sq", [128, 1, KC, TB], BF16)
        self.sqb = [[Buf(f"sq{i}_{kc}") for kc in range(KC)] for i in range(1)]
        self.rstd = nc.alloc_sbuf_tensor("rstd", [128, 2, TB], F32)
        self.rstdb = [Buf(f"rstd{i}") for i in range(2)]
        self.sg = nc.alloc_sbuf_tensor("sg", [128, 3, TB], F32)
        self.sgb = [Buf(f"sg{i}") for i in range(3)]
        self.sg_i = 0
        self.cvp = nc.alloc_sbuf_tensor("cvp_sb", [128, 8 * 37], F32)
        self.sel = nc.alloc_sbuf_tensor("sel_sb", [128, 9], F32)
        self.ones32 = nc.alloc_sbuf_tensor("ones32", [128, 128], F32)
        self.vmo = nc.alloc_sbuf_tensor("vmo", [128, 128], BF16)
        self.vmp = nc.alloc_sbuf_tensor("vmp", [128, 128], BF16)
        self.cc_sem = P.dma_sem("cc")
        self.x_sem = P.dma_sem("d_xch")
        self.st_sems = [P.dma_sem("d_st0"), P.dma_sem("d_st1")]
        self.w1_sem = P.dma_sem("d_w1")
        self.w2_sem = P.dma_sem("d_w2")
        self.tab_sem = P.dma_sem("d_tab")
        self.ps = [nc.alloc_psum_tensor(f"ps{i}", [128, TB], F32) for i in range(8)]
        self.psb = [Buf(f"ps{i}") for i in range(8)]
        self.ps_i = 0
        self.misc_sem = P.dma_sem("d_misc")
        self.out_sem = P.dma_sem("d_out")
        self.build()

    def bank(self):
        i = self.ps_i
        self.ps_i = (i + 1) % 8
        return self.ps[i], self.psb[i]

    def load_consts(self):
        P = self.P
        x, xT_in = self.x, self.xT_in
        for kc in range(KC):
            P.dma(P.sp, lambda e, kc=kc: e.dma_start(out=x[:, kc, :], in_=xT_in[kc * 128:(kc + 1) * 128, :]),
                  self.misc_sem, writes=self.xb[kc], final=16 * (KC + 3))
        P.dma(P.sp, lambda e: e.dma_start(out=self.gains[:, :], in_=self.gains_in[:, :]),
              self.misc_sem, writes=[self.gains_b], final=16 * (KC + 3))
        NM = KC + 3
        P.dma(P.sp, lambda e: e.dma_start(out=self.cvp[:, :], in_=self.cvp_in[:, :]),
              self.misc_sem, writes=[Buf("cvp")], final=16 * NM)
        P.dma(P.sp, lambda e: e.dma_start(out=self.sel[:, :], in_=self.sel_in[:, :]),
              self.misc_sem, writes=[Buf("sel")], final=16 * NM)
        P.op(P.dve, lambda e: e.memset(self.ones32[:, :], 1.0 / 512), writes=[self.ones_b], inc=False)
        P.op(P.dve, lambda e: e.memset(self.vmo[:, :], 1.0), writes=[self.ones_b], inc=False)
        P.op(P.dve, lambda e: e.memset(self.ones[:, :], 1.0 / D), writes=[self.ones_b], inc=False)
        P.op(P.dve, lambda e: e.memset(self.eps_rms[:, :], RMS_EPS), writes=[self.ones_b], inc=False)
        P.op(P.dve, lambda e: e.memset(self.eps_ln[:, :], LN_EPS), writes=[self.ones_b])

    def rmsnorm(self, nidx):
        P = self.P
        x, hT = self.x, self.hT
        for tb in range(NTB):
            ts = slice(tb * TB, (tb + 1) * TB)
            si = tb % 2
            for kc in range(KC):
                P.op(P.act, lambda e, kc=kc, ts=ts, si=si: e.activation(
                    out=self.sq[:, 0, kc, :], in_=x[:, kc, ts], func=AF.Square),
                    reads=[self.xb[kc][tb]], writes=[self.sqb[0][kc]])
            ps, psb = self.bank()
            for kc in range(KC):
                P.op(P.pe, lambda e, kc=kc, si=si, ps=ps: e.matmul(
                    ps[:, :], lhsT=self.ones[:, :], rhs=self.sq[:, 0, kc, :], start=(kc == 0), stop=(kc == KC - 1)),
                    reads=[self.ones_b, self.sqb[0][kc]], writes=[psb], inc=(kc == KC - 1))
            P.op(P.act, lambda e, si=si, ps=ps: e.activation(
                out=self.rstd[:, si, :], in_=ps[:, :], func=AF.Sqrt, bias=self.eps_rms[:, :], scale=1.0),
                reads=[psb, self.ones_b], writes=[self.rstdb[si]])
            P.op(P.dve, lambda e, si=si: e.reciprocal(out=self.rstd[:, si, :], in_=self.rstd[:, si, :]),
                reads=[self.rstdb[si]], writes=[self.rstdb[si]])
            for kc in range(KC):
                g = self.gains[:, nidx * KC + kc: nidx * KC + kc + 1]
                P.op(P.dve, lambda e, kc=kc, ts=ts, si=si, g=g: e.scalar_tensor_tensor(
                    out=hT[:, kc, ts], in0=x[:, kc, ts], scalar=g, in1=self.rstd[:, si, :],
                    op0=ALU.mult, op1=ALU.mult),
                    reads=[self.xb[kc][tb], self.rstdb[si], self.gains_b], writes=[self.hb[kc][tb]])

    def ffn(self, fidx, nidx):
        P = self.P
        x, hT, actT = self.x, self.hT, self.actT
        self.rmsnorm(nidx)
        j0 = 0
        for G in FF_GROUPS:
            gu_slots = []
            for jj in range(G):
                s = self.wgu_i
                self.wgu_i = (s + 1) % self.NGU
                gu_slots.append(s)
            dn_slots = []
            for jj in range(G):
                s = self.wdn_i
                self.wdn_i = (s + 1) % self.NDN
                dn_slots.append(s)
            for jj in range(G):
                s = gu_slots[jj]
                src = self.wgu_in[fidx * NJ + j0 + jj]
                P.dma(P.pool, lambda e, s=s, src=src: e.dma_start(out=self.wgu[:, s, :], in_=src),
                      self.wgus[s], writes=[self.wgub[s]])
                for tb in range(NTB):
                    ts = slice(tb * TB, (tb + 1) * TB)
                    pg, pgb = self.bank()
                    pu, pub = self.bank()
                    for half, (pp, ppb) in enumerate(((pg, pgb), (pu, pub))):
                        for kc in range(KC):
                            w = self.wgu[:, s, kc * 256 + half * 128: kc * 256 + half * 128 + 128]
                            P.op(P.pe, lambda e, pp=pp, w=w, kc=kc, ts=ts: e.matmul(
                                pp[:, :], lhsT=w, rhs=hT[:, kc, ts], start=(kc == 0), stop=(kc == KC - 1)),
                                reads=[self.wgub[s], self.hb[kc][tb]], writes=[ppb], inc=(kc == KC - 1))
                    si = self.sg_i
                    self.sg_i = (si + 1) % 3
                    P.op(P.act, lambda e, pg=pg, si=si: e.activation(out=self.sg[:, si, :], in_=pg[:, :], func=AF.Silu),
                         reads=[pgb], writes=[self.sgb[si]])
                    P.op(P.dve, lambda e, pu=pu, si=si, jj=jj, ts=ts: e.tensor_tensor(
                        out=actT[:, jj, ts], in0=self.sg[:, si, :], in1=pu[:, :], op=ALU.mult),
                        reads=[self.sgb[si], pub], writes=[self.actb[jj][tb]])
            for jj in range(G):
                s = dn_slots[jj]
                src = self.wdn_in[fidx * NJ + j0 + jj]
                P.dma(P.pool, lambda e, s=s, src=src: e.dma_start(out=self.wdn[:, s, :], in_=src),
                      self.wdns[s], writes=[self.wdnb[s]])
            for c in range(KC):
                for tb in range(NTB):
                    ts = slice(tb * TB, (tb + 1) * TB)
                    po, pob = self.bank()
                    for jj in range(G):
                        s = dn_slots[jj]
                        P.op(P.pe, lambda e, po=po, s=s, c=c, jj=jj, ts=ts, G=G: e.matmul(
                            po[:, :], lhsT=self.wdn[:, s, c * 128:(c + 1) * 128], rhs=actT[:, jj, ts],
                            start=(jj == 0), stop=(jj == G - 1)),
                            reads=[self.wdnb[s], self.actb[jj][tb]], writes=[pob], inc=(jj == G - 1))
                    P.op(P.dve, lambda e, po=po, c=c, ts=ts: e.scalar_tensor_tensor(
                        out=x[:, c, ts], in0=po[:, :], scalar=0.5, in1=x[:, c, ts], op0=ALU.mult, op1=ALU.add),
                        reads=[pob, self.xb[c][tb]], writes=[self.xb[c][tb]])
            j0 += G

    def exchange(self, tag, ncols, prev, prev_b):
        P = self.P
        xin, xag, xin_b, xag_b = self.xch[tag]
        if self.single:
            P.op(P.dve, lambda e: e.memset(prev, 0.0), writes=[prev_b])
            return
        hb_all = [b for row in self.hb for b in row]
        P.dma(P.sp, lambda e: e.dma_start(out=xin.ap().rearrange("p (k t) -> p k t", k=KC), in_=self.hT[:, :, T - ncols:T]),
              self.x_sem, reads=hb_all, writes=[xin_b])
        P.coll(lambda e: e.collective_compute("AllGather", ALU.bypass, replica_groups=[list(range(NCORES))],
                                              ins=[xin.ap().opt()], outs=[xag.ap().opt()]),
               self.cc_sem, reads=[xin_b], writes=[xag_b])
        i = 0
        for kc in range(KC):
            for r in range(NCORES):
                st = self.stage[i % 2]
                stb = self.stage_b[i % 2]
                src = xag.ap()[r * 128:(r + 1) * 128, kc * ncols:(kc + 1) * ncols]
                P.dma(P.sp, lambda e, st=st, src=src: e.dma_start(out=st[:, 0:ncols], in_=src),
                      self.st_sems[i % 2], reads=[xag_b], writes=[stb])
                if r == 0:
                    P.op(P.dve, lambda e, st=st, kc=kc, r=r: e.tensor_scalar(
                        out=prev[:, kc, :], in0=st[:, 0:ncols], scalar1=self.sel[:, r:r + 1], scalar2=None, op0=ALU.mult),
                        reads=[stb, self.gains_b], writes=[prev_b])
                else:
                    P.op(P.dve, lambda e, st=st, kc=kc, r=r: e.scalar_tensor_tensor(
                        out=prev[:, kc, :], in0=st[:, 0:ncols], scalar=self.sel[:, r:r + 1], in1=prev[:, kc, :],
                        op0=ALU.mult, op1=ALU.add),
                        reads=[stb, self.gains_b], writes=[prev_b])
                i += 1

    def proj_residual(self, w, wb, rhs, rhs_b, scale=1.0):
        P = self.P
        x = self.x
        for c in range(KC):
            for tb in range(NTB):
                ts = slice(tb * TB, (tb + 1) * TB)
                po, pob = self.bank()
                P.op(P.pe, lambda e, po=po, c=c, ts=ts: e.matmul(
                    po[:, :], lhsT=w[:, c * 128:(c + 1) * 128], rhs=rhs[:, ts], start=True, stop=True),
                    reads=[wb, rhs_b], writes=[pob])
                P.op(P.dve, lambda e, po=po, c=c, ts=ts: e.scalar_tensor_tensor(
                    out=x[:, c, ts], in0=po[:, :], scalar=scale, in1=x[:, c, ts], op0=ALU.mult, op1=ALU.add),
                    reads=[pob, self.xb[c][tb]], writes=[self.xb[c][tb]])

    def conv(self, ci, nidx):
        P = self.P
        ar = self.arena
        hT = self.hT
        self.rmsnorm(nidx)
        o = 0
        def take(n):
            nonlocal o
            v = ar[:, o:o + n]
            o += n
            return v
        wcv = take(KC * 640).rearrange("p (k n) -> p k n", k=KC); wcv_b = Buf("wcv")
        cx = take(2 * 2056).bitcast(F32); cx_b = Buf("cx")
        u = take(2 * 2080).bitcast(F32); u_b = Buf("u")
        tmp = take(2 * 2 * TB).bitcast(F32).rearrange("p (i t) -> p i t", i=2); tmp_b = [Buf("tmp0"), Buf("tmp1")]
        v = take(2 * 4 * T).bitcast(F32).rearrange("p (q t) -> p q t", q=4); v_b = [[Buf(f"v{q}_{tb}") for tb in range(NTB)] for q in range(4)]
        ya = take(T); ya_b = Buf("ya")
        wout = take(2 * D).rearrange("p (i d) -> p i d", i=2); wout_b = [Buf("wo0"), Buf("wo1")]
        hprev = take(KC * 32).rearrange("p (k t) -> p k t", k=KC); hprev_b = Buf("hprev")
        self.stage = [take(KC * 32), take(KC * 32)]; self.stage_b = [Buf("st0"), Buf("st1")]
        tmp4 = take(2 * 4 * TB).bitcast(F32).rearrange("p (q t) -> p q t", q=4); tmp4_b = [Buf(f"t4{q}") for q in range(4)]
        yb = take(4 * TB).rearrange("p (q t) -> p q t", q=4); yb_b = [Buf(f"yb{q}") for q in range(4)]
        acc = take(2 * TB).bitcast(F32); acc_b = Buf("acc")
        assert o <= self.AR
        self.exchange(f"c{ci}", 32, hprev, hprev_b)
        wo_i = 0
        for q in range(4):
            pc = (ci * 4 + q) * 37
            cp = lambda j, pc=pc: self.cvp[:, pc + j:pc + j + 1]
            P.dma(P.pool, lambda e, q=q: e.dma_start(out=wcv, in_=self.wcv_in[ci * 4 + q].rearrange("p (k n) -> p k n", k=KC)),
                  self.w1_sem, writes=[wcv_b])
            ph, phb = self.bank()
            for g in range(5):
                for kc in range(KC):
                    P.op(P.pe, lambda e, g=g, kc=kc, ph=ph: e.matmul(
                        ph[:, g * 32:(g + 1) * 32], lhsT=wcv[:, kc, g * 128:(g + 1) * 128], rhs=hprev[:, kc, :],
                        start=(kc == 0), stop=(kc == KC - 1)),
                        reads=[wcv_b, hprev_b], writes=[phb], inc=(kc == KC - 1))
            P.op(P.act, lambda e, ph=ph: e.activation(out=tmp[:, 0, 0:32], in_=ph[:, 32:64], func=AF.Copy),
                 reads=[phb], writes=[tmp_b[0]])
            P.op(P.dve, lambda e, ph=ph: e.tensor_tensor(out=cx[:, 0:2], in0=tmp[:, 0, 30:32], in1=ph[:, 94:96], op=ALU.mult),
                 reads=[phb, tmp_b[0]], writes=[cx_b])
            P.op(P.act, lambda e, ph=ph: e.activation(out=tmp[:, 1, 0:32], in_=ph[:, 128:160], func=AF.Sigmoid),
                 reads=[phb], writes=[tmp_b[1]])
            P.op(P.dve, lambda e, ph=ph: e.tensor_tensor(out=u[:, 0:30], in0=tmp[:, 1, 2:32], in1=ph[:, 98:128], op=ALU.mult),
                 reads=[phb, tmp_b[1]], writes=[u_b])
            for tb in range(NTB):
                ts = slice(tb * TB, (tb + 1) * TB)
                pz = []
                for g in range(5):
                    pp, ppb = self.bank()
                    pz.append((pp, ppb))
                    for kc in range(KC):
                        P.op(P.pe, lambda e, g=g, kc=kc, pp=pp, ts=ts: e.matmul(
                            pp[:, :], lhsT=wcv[:, kc, g * 128:(g + 1) * 128], rhs=hT[:, kc, ts],
                            start=(kc == 0), stop=(kc == KC - 1)),
                            reads=[wcv_b, self.hb[kc][tb]], writes=[ppb], inc=(kc == KC - 1))
                (pab, pabb), (pac, pacb), (pax, paxb), (pbv, pbvb), (pbg, pbgb) = pz
                P.op(P.act, lambda e, pac=pac: e.activation(out=tmp[:, 0, :], in_=pac[:, :], func=AF.Copy),
                     reads=[pacb], writes=[tmp_b[0]])
                P.op(P.dve, lambda e, pax=pax, tb=tb: e.tensor_tensor(
                    out=cx[:, 2 + tb * TB:2 + (tb + 1) * TB], in0=tmp[:, 0, :], in1=pax[:, :], op=ALU.mult),
                    reads=[paxb, tmp_b[0]], writes=[cx_b])
                P.op(P.dve, lambda e, tb=tb, cp=cp: e.tensor_scalar(
                    out=acc[:, :], in0=cx[:, tb * TB:tb * TB + TB], scalar1=cp(0), scalar2=None, op0=ALU.mult),
                    reads=[cx_b, self.gains_b], writes=[acc_b])
                for k in (1, 2):
                    P.op(P.dve, lambda e, tb=tb, k=k, cp=cp: e.scalar_tensor_tensor(
                        out=acc[:, :], in0=cx[:, tb * TB + k:tb * TB + k + TB], scalar=cp(k), in1=acc[:, :],
                        op0=ALU.mult, op1=ALU.add),
                        reads=[cx_b, acc_b], writes=[acc_b])
                P.op(P.dve, lambda e, pab=pab, ts=ts: e.tensor_tensor(out=ya[:, ts], in0=acc[:, :], in1=pab[:, :], op=ALU.mult),
                     reads=[acc_b, pabb], writes=[ya_b])
                P.op(P.act, lambda e, pbg=pbg: e.activation(out=tmp[:, 1, :], in_=pbg[:, :], func=AF.Sigmoid),
                     reads=[pbgb], writes=[tmp_b[1]])
                P.op(P.dve, lambda e, pbv=pbv, tb=tb: e.tensor_tensor(
                    out=u[:, 30 + tb * TB:30 + (tb + 1) * TB], in0=tmp[:, 1, :], in1=pbv[:, :], op=ALU.mult),
                    reads=[pbvb, tmp_b[1]], writes=[u_b])
                P.op(P.dve, lambda e, tb=tb, q=q, ts=ts, cp=cp: e.tensor_scalar(
                    out=v[:, q, ts], in0=u[:, tb * TB:tb * TB + TB], scalar1=cp(3), scalar2=cp(34), op0=ALU.mult, op1=ALU.add),
                    reads=[u_b, self.gains_b], writes=[v_b[q][tb]])
                for k in range(1, 31):
                    P.op(P.dve, lambda e, tb=tb, k=k, q=q, ts=ts, cp=cp: e.scalar_tensor_tensor(
                        out=v[:, q, ts], in0=u[:, tb * TB + k:tb * TB + k + TB], scalar=cp(3 + k), in1=v[:, q, ts],
                        op0=ALU.mult, op1=ALU.add),
                        reads=[u_b, v_b[q][tb]], writes=[v_b[q][tb]])
            wi = 0
            wo_i += 1
            P.dma(P.pool, lambda e, wi=wi, q=q: e.dma_start(out=wout[:, wi, :], in_=self.wco_in[ci * 8 + q]),
                  self.w2_sem, writes=[wout_b[wi]])
            self.proj_residual(wout[:, wi, :], wout_b[wi], ya, ya_b)
        wv = ar[:, 0:4 * D].rearrange("p (q d) -> p q d", q=4)
        wv_b = wcv_b
        P.dma(P.pool, lambda e: e.dma_start(out=wv, in_=self.wco_in[ci * 8 + 4:ci * 8 + 8].rearrange("q p d -> p q d")),
              self.w1_sem, writes=[wv_b])
        for tb in range(NTB):
            ts = slice(tb * TB, (tb + 1) * TB)
            pm, pmb = self.bank()
            for q in range(4):
                P.op(P.pe, lambda e, q=q, pm=pm, ts=ts: e.matmul(
                    pm[:, :], lhsT=self.ones32[:, :], rhs=v[:, q, ts], start=(q == 0), stop=(q == 3)),
                    reads=[self.ones_b, v_b[q][tb]], writes=[pmb], inc=(q == 3))
            for q in range(4):
                P.op(P.act, lambda e, q=q, ts=ts: e.activation(out=tmp4[:, q, :], in_=v[:, q, ts], func=AF.Square),
                     reads=[v_b[q][tb]], writes=[tmp4_b[q]])
            pq, pqb = self.bank()
            for q in range(4):
                P.op(P.pe, lambda e, q=q, pq=pq: e.matmul(
                    pq[:, :], lhsT=self.ones32[:, :], rhs=tmp4[:, q, :], start=(q == 0), stop=(q == 3)),
                    reads=[self.ones_b, tmp4_b[q]], writes=[pqb], inc=(q == 3))
            P.op(P.act, lambda e, pm=pm: e.activation(out=tmp[:, 0, :], in_=pm[:, :], func=AF.Square),
                 reads=[pmb], writes=[tmp_b[0]])
            P.op(P.dve, lambda e, pq=pq: e.tensor_tensor(out=tmp[:, 0, :], in0=pq[:, :], in1=tmp[:, 0, :], op=ALU.subtract),
                 reads=[pqb, tmp_b[0]], writes=[tmp_b[0]])
            P.op(P.act, lambda e: e.activation(out=tmp[:, 0, :], in_=tmp[:, 0, :], func=AF.Sqrt, bias=self.eps_ln[:, :], scale=1.0),
                 reads=[tmp_b[0], self.ones_b], writes=[tmp_b[0]])
            P.op(P.dve, lambda e: e.reciprocal(out=tmp[:, 0, :], in_=tmp[:, 0, :]), reads=[tmp_b[0]], writes=[tmp_b[0]])
            for q in range(4):
                pc = (ci * 4 + q) * 37
                P.op(P.dve, lambda e, q=q, ts=ts, pm=pm: e.tensor_tensor(out=tmp4[:, q, :], in0=v[:, q, ts], in1=pm[:, :], op=ALU.subtract),
                     reads=[v_b[q][tb], pmb], writes=[tmp4_b[q]])
                P.op(P.dve, lambda e, q=q: e.tensor_tensor(out=tmp4[:, q, :], in0=tmp4[:, q, :], in1=tmp[:, 0, :], op=ALU.mult),
                     reads=[tmp4_b[q], tmp_b[0]], writes=[tmp4_b[q]])
                P.op(P.act, lambda e, q=q, pc=pc: e.activation(out=yb[:, q, :], in_=tmp4[:, q, :], func=AF.Silu,
                                                            bias=self.cvp[:, pc + 36:pc + 37], scale=self.cvp[:, pc + 35:pc + 36]),
                     reads=[tmp4_b[q], self.gains_b], writes=[yb_b[q]])
            for c in range(KC):
                po, pob = self.bank()
                for q in range(4):
                    P.op(P.pe, lambda e, po=po, c=c, q=q: e.matmul(
                        po[:, :], lhsT=wv[:, q, c * 128:(c + 1) * 128], rhs=yb[:, q, :], start=(q == 0), stop=(q == 3)),
                        reads=[wv_b, yb_b[q]], writes=[pob], inc=(q == 3))
                P.op(P.dve, lambda e, po=po, c=c, ts=ts: e.tensor_tensor(out=self.x[:, c, ts], in0=po[:, :], in1=self.x[:, c, ts], op=ALU.add),
                     reads=[pob, self.xb[c][tb]], writes=[self.xb[c][tb]])

    def attn(self, ai, nidx):
        P = self.P
        ar = self.arena
        hT = self.hT
        self.rmsnorm(nidx)
        o = 0
        def take(n):
            nonlocal o
            v = ar[:, o:o + n]
            o += n
            return v
        hprev = take(KC * T).rearrange("p (k t) -> p k t", k=KC); hprev_b = Buf("hprev")
        Qm = take(T); Qm_b = Buf("Qm")
        Km = take(2 * T); Km_b = Buf("Km")
        Vm = take(32 * 128).rearrange("p (k d) -> p k d", k=32); Vm_b = Buf("Vm")
        tab = take(2 * 17 * 128).bitcast(F32); tab_b = Buf("tab")
        OTm = take(T); OTm_b = Buf("OTm")
        wqkv = take(KC * 384).rearrange("p (k n) -> p k n", k=KC); wqkv_b = Buf("wqkv")
        wo = take(D); wo_b = Buf("wo")
        sT = take(2 * 2 * TB).bitcast(F32).rearrange("p (i t) -> p i t", i=2); sT_b = [Buf("sT0"), Buf("sT1")]
        pT = take(2 * TB).rearrange("p (i t) -> p i t", i=2); pT_b = [Buf("pT0"), Buf("pT1")]
        rdn = take(2 * 128).bitcast(F32); rdn_b = Buf("rdn")
        self.stage = [take(T), take(T)]; self.stage_b = [Buf("st0"), Buf("st1")]
        assert o <= self.AR, o
        self.exchange(f"a{ai}", T, hprev, hprev_b)
        P.op(P.dve, lambda e: e.tensor_scalar(out=self.vmp[:, :], in0=self.vmo[:, :], scalar1=self.sel[:, 8:9], scalar2=None, op0=ALU.mult),
             reads=[self.ones_b, self.gains_b], writes=[self.ones_b])
        hprev_bufs = [hprev_b]
        it = 0
        for m in range(8):
            P.dma(P.pool, lambda e, m=m: e.dma_start(out=wqkv, in_=self.wqkv_in[ai * 8 + m].rearrange("p (k n) -> p k n", k=KC)),
                  self.w1_sem, writes=[wqkv_b])
            P.dma(P.pool, lambda e, m=m: e.dma_start(out=wo, in_=self.wo_in[ai * 8 + m]), self.w2_sem, writes=[wo_b])
            for tb in range(NTB):
                ts = slice(tb * TB, (tb + 1) * TB)
                pp, ppb = self.bank()
                for kc in range(KC):
                    P.op(P.pe, lambda e, kc=kc, pp=pp, ts=ts: e.matmul(
                        pp[:, :], lhsT=wqkv[:, kc, 0:128], rhs=hT[:, kc, ts], start=(kc == 0), stop=(kc == KC - 1)),
                        reads=[wqkv_b, self.hb[kc][tb]], writes=[ppb], inc=(kc == KC - 1))
                P.op(P.act, lambda e, pp=pp, ts=ts: e.activation(out=Qm[:, ts], in_=pp[:, :], func=AF.Copy),
                     reads=[ppb], writes=[Qm_b])
            for eb in range(8):
                pp, ppb = self.bank()
                for kc in range(KC):
                    if eb < 4:
                        rhs = hprev[:, kc, eb * TB:(eb + 1) * TB]; rb = hprev_b
                    else:
                        rhs = hT[:, kc, (eb - 4) * TB:(eb - 3) * TB]; rb = self.hb[kc][eb - 4]
                    P.op(P.pe, lambda e, kc=kc, pp=pp, rhs=rhs: e.matmul(
                        pp[:, :], lhsT=wqkv[:, kc, 128:256], rhs=rhs, start=(kc == 0), stop=(kc == KC - 1)),
                        reads=[wqkv_b, rb], writes=[ppb], inc=(kc == KC - 1))
                P.op(P.act, lambda e, pp=pp, eb=eb: e.activation(out=Km[:, eb * TB:(eb + 1) * TB], in_=pp[:, :], func=AF.Copy),
                     reads=[ppb], writes=[Km_b])
            for k4 in range(8):
                pp, ppb = self.bank()
                for j in range(4):
                    kb = k4 * 4 + j
                    for kc in range(KC):
                        if kb < 16:
                            lh = hprev[:, kc, kb * 128:(kb + 1) * 128]; rb = hprev_b
                        else:
                            lh = hT[:, kc, (kb - 16) * 128:(kb - 15) * 128]; rb = self.hb[kc][(kb - 16) // 4]
                        P.op(P.pe, lambda e, kc=kc, pp=pp, lh=lh, j=j: e.matmul(
                            pp[:, j * 128:(j + 1) * 128], lhsT=lh, rhs=wqkv[:, kc, 256:384], start=(kc == 0), stop=(kc == KC - 1)),
                            reads=[wqkv_b, rb], writes=[ppb], inc=(kc == KC - 1))
                P.op(P.act, lambda e, pp=pp, k4=k4: e.activation(
                    out=Vm[:, k4 * 4:(k4 + 1) * 4, :], in_=pp[:, :].rearrange("p (j d) -> p j d", j=4), func=AF.Copy),
                    reads=[ppb], writes=[Vm_b])
            for hh in range(2):
                head = 2 * m + hh
                hs = slice(hh * 64, (hh + 1) * 64)
                P.dma(P.sp, lambda e, head=head: e.dma_start(out=tab, in_=self.tab_in[head]), self.tab_sem, writes=[tab_b])
                for qb in range(16):
                    qs = slice(qb * 128, (qb + 1) * 128)
                    pu, pub = self.bank()
                    pd, pdb = self.bank()
                    for j0 in range(0, 17, 4):
                        n = min(4, 17 - j0)
                        psS, psSb = self.bank()
                        for jj in range(n):
                            kbe = qb + j0 + jj
                            P.op(P.pe, lambda e, psS=psS, jj=jj, kbe=kbe, hs=hs, qs=qs: e.matmul(
                                psS[:, jj * 128:(jj + 1) * 128], lhsT=Km[hs, kbe * 128:(kbe + 1) * 128], rhs=Qm[hs, qs],
                                start=True, stop=True),
                                reads=[Km_b, Qm_b], writes=[psSb], inc=(jj == n - 1))
                        si = it % 2
                        it += 1
                        P.op(P.dve, lambda e, psS=psS, si=si, j0=j0, n=n: e.scalar_tensor_tensor(
                            out=sT[:, si, 0:n * 128], in0=psS[:, 0:n * 128], scalar=0.125, in1=tab[:, j0 * 128:(j0 + n) * 128],
                            op0=ALU.mult, op1=ALU.add),
                            reads=[psSb, tab_b], writes=[sT_b[si]])
                        P.op(P.act, lambda e, si=si, n=n: e.activation(out=pT[:, si, 0:n * 128], in_=sT[:, si, 0:n * 128], func=AF.Exp),
                             reads=[sT_b[si]], writes=[pT_b[si]])
                        for jj in range(n):
                            j = j0 + jj
                            kbe = qb + j
                            vmat = self.vmp if kbe < 16 else self.vmo
                            P.op(P.pe, lambda e, pu=pu, kbe=kbe, si=si, jj=jj, j=j: e.matmul(
                                pu[:, 0:128], lhsT=Vm[:, kbe, :], rhs=pT[:, si, jj * 128:(jj + 1) * 128], start=(j == 0), stop=(j == 16)),
                                reads=[Vm_b, pT_b[si]], writes=[pub], inc=False)
                            P.op(P.pe, lambda e, pd=pd, vmat=vmat, si=si, jj=jj, j=j: e.matmul(
                                pd[:, 0:128], lhsT=vmat[:, :], rhs=pT[:, si, jj * 128:(jj + 1) * 128], start=(j == 0), stop=(j == 16)),
                                reads=[self.ones_b, pT_b[si]], writes=[pdb], inc=(jj == n - 1))
                    P.op(P.act, lambda e, pd=pd, hs=hs: e.activation(out=rdn[hs, :], in_=pd[hs, 0:128], func=AF.Copy),
                         reads=[pdb], writes=[rdn_b])
                    P.op(P.dve, lambda e, hs=hs: e.reciprocal(out=rdn[hs, :], in_=rdn[hs, :]), reads=[rdn_b], writes=[rdn_b])
                    P.op(P.dve, lambda e, pu=pu, hs=hs, qs=qs: e.tensor_tensor(out=OTm[hs, qs], in0=pu[hs, 0:128], in1=rdn[hs, :], op=ALU.mult),
                         reads=[pub, rdn_b], writes=[OTm_b])
            self.proj_residual(wo, wo_b, OTm, OTm_b)

    def final_norm(self):
        P = self.P
        x = self.x
        nidx = 12
        for tb in range(NTB):
            ts = slice(tb * TB, (tb + 1) * TB)
            si = tb % 2
            for kc in range(KC):
                P.op(P.act, lambda e, kc=kc, ts=ts, si=si: e.activation(
                    out=self.sq[:, 0, kc, :], in_=x[:, kc, ts], func=AF.Square),
                    reads=[self.xb[kc][tb]], writes=[self.sqb[0][kc]])
            ps, psb = self.bank()
            for kc in range(KC):
                P.op(P.pe, lambda e, kc=kc, si=si, ps=ps: e.matmul(
                    ps[:, :], lhsT=self.ones[:, :], rhs=self.sq[:, 0, kc, :], start=(kc == 0), stop=(kc == KC - 1)),
                    reads=[self.ones_b, self.sqb[0][kc]], writes=[psb], inc=(kc == KC - 1))
            P.op(P.act, lambda e, si=si, ps=ps: e.activation(
                out=self.rstd[:, si, :], in_=ps[:, :], func=AF.Sqrt, bias=self.eps_rms[:, :], scale=1.0),
                reads=[psb, self.ones_b], writes=[self.rstdb[si]])
            P.op(P.dve, lambda e, si=si: e.reciprocal(out=self.rstd[:, si, :], in_=self.rstd[:, si, :]),
                reads=[self.rstdb[si]], writes=[self.rstdb[si]])
            for kc in range(KC):
                g = self.gains[:, nidx * KC + kc: nidx * KC + kc + 1]
                P.op(P.dve, lambda e, kc=kc, ts=ts, si=si, g=g: e.scalar_tensor_tensor(
                    out=x[:, kc, ts], in0=x[:, kc, ts], scalar=g, in1=self.rstd[:, si, :],
                    op0=ALU.mult, op1=ALU.mult),
                    reads=[self.xb[kc][tb], self.rstdb[si], self.gains_b], writes=[self.xb[kc][tb]])

    def store_x(self):
        P = self.P
        for kc in range(KC):
            P.dma(P.sp, lambda e, kc=kc: e.dma_start(out=self.out_ap[kc * 128:(kc + 1) * 128, :], in_=self.x[:, kc, :]),
                  self.out_sem, reads=self.xb[kc], final=16 * KC)

    def build(self):
        P = self.P
        self.load_consts()
        for ph in self.phases:
            kind = ph[0]
            if kind == "ffn":
                self.ffn(ph[1], ph[2])
            elif kind == "final":
                self.final_norm()
            elif kind == "conv":
                self.conv(ph[1], ph[2])
            elif kind == "attn":
                self.attn(ph[1], ph[2])
            P.barrier(P.engs)
        self.store_x()
        P.finish([(self.out_sem, self.out_sem.count)])


def prep_weights(inp):
    f = lambda k: np.asarray(inp[k], np.float32)
    gains = np.stack([f("ffn1_norm")[l] for l in range(4)] + [f("mix_norm")[l] for l in range(4)]
                     + [f("ffn2_norm")[l] for l in range(4)] + [f("final_norm")], 0)
    gains = np.ascontiguousarray(gains.reshape(13, KC, 128).transpose(2, 0, 1).reshape(128, 13 * KC))
    wgu = np.empty((8, NJ, 128, KC, 256), np.float32)
    wdn = np.empty((8, NJ, 128, D), np.float32)
    for l in range(4):
        for w, (kgu, kdn) in enumerate((("ffn1_w_gate_up", "ffn1_w_down"), ("ffn2_w_gate_up", "ffn2_w_down"))):
            fi = l * 2 + w
            gu = f(kgu)[l]
            g = gu[:, :DFF].reshape(KC, 128, NJ, 128)
            u = gu[:, DFF:].reshape(KC, 128, NJ, 128)
            wgu[fi, :, :, :, :128] = g.transpose(2, 1, 0, 3)
            wgu[fi, :, :, :, 128:] = u.transpose(2, 1, 0, 3)
            wdn[fi] = f(kdn)[l].reshape(NJ, 128, D)
    wcv = np.empty((2, 4, 128, KC, 5, 128), np.float32)
    wco = np.empty((2, 8, 128, D), np.float32)
    cvp = np.empty((128, 2, 4, 37), np.float32)
    for ci in range(2):
        win = f("conv_w_in")[ci].reshape(KC, 128, 5, 4, 128)
        wcv[ci] = win.transpose(3, 1, 0, 2, 4)
        wco[ci] = f("conv_w_out")[ci].reshape(8, 128, D)
        ak = f("conv_a_kernel")[ci].reshape(3, 4, 128)
        bk = f("conv_b_kernel")[ci].reshape(31, 4, 128)
        cvp[:, ci, :, 0:3] = ak.transpose(2, 1, 0)
        cvp[:, ci, :, 3:34] = bk.transpose(2, 1, 0)
        cvp[:, ci, :, 34] = f("conv_b_bias")[ci].reshape(4, 128).T
        cvp[:, ci, :, 35] = f("conv_b_ln_gain")[ci].reshape(4, 128).T
        cvp[:, ci, :, 36] = f("conv_b_ln_bias")[ci].reshape(4, 128).T
    wqkv = np.empty((2, 8, 128, KC, 3, 128), np.float32)
    wo = np.empty((2, 8, 128, D), np.float32)
    for ai in range(2):
        w = f("attn_w_qkv")[ai].reshape(KC, 128, 3, 8, 128)
        wqkv[ai] = w.transpose(3, 1, 0, 2, 4)
        wo[ai] = f("attn_w_o")[ai].reshape(8, 128, D)
    k = np.arange(128)[:, None, None]
    j = np.arange(17)[None, :, None]
    q = np.arange(128)[None, None, :]
    dlt = (16 - j) * 128 + q - k
    mult = ((dlt <= 128).astype(np.float64) + ((dlt % 4 == 0) & (dlt <= 512)) + ((dlt % 16 == 0) & (dlt <= 2048)))
    ok = (dlt >= 0) & (mult > 0)
    lnm = np.log(np.where(ok, mult, 1.0))
    tab = np.empty((16, 128, 17 * 128), np.float32)
    for h in range(16):
        slope = 2.0 ** (-8.0 * (h + 1) / 16)
        tab[h] = np.where(ok, -slope * dlt + lnm, -30000.0).reshape(128, 17 * 128)
    return dict(gains=gains, wgu=wgu.reshape(8 * NJ, 128, KC * 256), wdn=wdn.reshape(8 * NJ, 128, D),
                wcv=wcv.reshape(8, 128, KC * 640), wco=wco.reshape(16, 128, D), cvp=cvp.reshape(128, 8 * 37),
                wqkv=wqkv.reshape(16, 128, KC * 384), wo=wo.reshape(16, 128, D), tab=tab)


def ffn_idx(layer, which):
    return layer * 2 + which


def full_phases():
    ph = []
    for l in range(DEPTH):
        ph.append(("ffn", ffn_idx(l, 0), l))
        if l % 2 == 0:
            ph.append(("conv", l // 2, 4 + l))
        else:
            ph.append(("attn", l // 2, 4 + l))
        ph.append(("ffn", ffn_idx(l, 1), 8 + l))
    ph.append(("final",))
    return ph


def make_in_maps(inp):
    x = np.asarray(inp["x"], np.float32)
    W = prep_weights(inp)
    in_maps = []
    for c in range(NCORES):
        b, ch = divmod(c, 4)
        xT = np.ascontiguousarray(x[b, ch * T:(ch + 1) * T, :].T)
        sel = np.zeros((128, 9), np.float32)
        if ch > 0:
            sel[:, c - 1] = 1.0
            sel[:, 8] = 1.0
        m = dict(W)
        m["xT"] = xT
        m["sel"] = sel
        in_maps.append(m)
    return in_maps


def run_phases(inp, phases):
    x = np.asarray(inp["x"], np.float32)
    mk = MK(phases)
    in_maps = make_in_maps(inp)
    res = run_bass_kernel_spmd(mk.nc, in_maps, core_ids=list(range(NCORES)))
    out = np.empty_like(x)
    for c in range(NCORES):
        b, ch = divmod(c, 4)
        out[b, ch * T:(ch + 1) * T, :] = np.asarray(res.results[c]["out"]).T
    return out


def kernel(**inputs):
    return run_phases(inputs, full_phases())
```

```python
import numpy as np
import ml_dtypes
import concourse.bass as bass
import concourse.mybir as mybir
from concourse.bass_utils import run_bass_kernel_spmd

F32 = mybir.dt.float32
BF16 = mybir.dt.bfloat16
ALU = mybir.AluOpType
AF = mybir.ActivationFunctionType

NCORES = 8
T = 2048
D = 1024
KC = 8
DFF = 2816
NJ = 22
TB = 512
NTB = T // TB
DEPTH = 4
RMS_EPS = 1e-6
LN_EPS = 1e-5
FF_GROUPS = [6, 6, 5, 5]


class Sem:
    def __init__(self, nc, name):
        self.h = nc.alloc_semaphore(name)
        self.count = 0
        self.name = name


class Buf:
    __slots__ = ("name", "w", "r")

    def __init__(self, name):
        self.name = name
        self.w = None
        self.r = []


class Eng:
    def __init__(self, nc, name, attr):
        self.name = name
        self.attr = attr
        self.sem = Sem(nc, "s_" + name)
        self.ops = []
        self.seen = {}

    def need(self, dep):
        if dep is None:
            return
        sem, val = dep
        if sem is self.sem and (self.name == "pe" or val > sem.count):
            return
        if self.seen.get(sem, 0) >= val:
            return
        self.seen[sem] = val
        h = sem.h
        self.ops.append(lambda e, h=h, val=val: e.wait_ge(h, val))


class Prog:
    def __init__(self, nc):
        self.nc = nc
        self.pe = Eng(nc, "pe", "tensor")
        self.act = Eng(nc, "act", "scalar")
        self.dve = Eng(nc, "dve", "vector")
        self.pool = Eng(nc, "pool", "gpsimd")
        self.sp = Eng(nc, "sp", "sync")
        self.engs = [self.pe, self.act, self.dve, self.pool, self.sp]
        self.dsems = []

    def dma_sem(self, name):
        s = Sem(self.nc, name)
        self.dsems.append(s)
        return s

    def op(self, eng, fn, reads=(), writes=(), inc=True):
        for b in reads:
            eng.need(b.w)
        for b in writes:
            eng.need(b.w)
            for d in b.r:
                eng.need(d)
        val = eng.sem.count + 1
        if inc:
            eng.sem.count = val
            h = eng.sem.h
            eng.ops.append(lambda e, fn=fn, h=h: fn(e).then_inc(h, 1))
        else:
            eng.ops.append(lambda e, fn=fn: fn(e))
        me = (eng.sem, val)
        for b in reads:
            b.r = [d for d in b.r if d[0] is not eng.sem] + [me]
        for b in writes:
            b.w = me
            b.r = []

    def dma(self, eng, fn, dsem, reads=(), writes=(), final=None):
        for b in reads:
            eng.need(b.w)
        for b in writes:
            eng.need(b.w)
            for d in b.r:
                eng.need(d)
        dsem.count += 16
        h = dsem.h
        eng.ops.append(lambda e, fn=fn, h=h: fn(e).then_inc(h, 16))
        me = (dsem, final if final is not None else dsem.count)
        for b in reads:
            b.r = b.r + [me]
        for b in writes:
            b.w = me
            b.r = []

    def coll(self, fn, csem, reads=(), writes=()):
        eng = self.pool
        for b in reads:
            eng.need(b.w)
        for b in writes:
            eng.need(b.w)
            for d in b.r:
                eng.need(d)
        csem.count += 1
        h = csem.h
        eng.ops.append(lambda e, fn=fn, h=h: fn(e).then_inc(h, 1))
        me = (csem, csem.count)
        for b in reads:
            b.r = b.r + [me]
        for b in writes:
            b.w = me
            b.r = []

    def barrier(self, engs=None):
        engs = engs or [self.pe, self.act, self.dve]
        for e in engs:
            for f in engs:
                if f is not e and f.sem.count > 0:
                    e.need((f.sem, f.sem.count))

    def finish(self, final_deps):
        for d in final_deps:
            self.sp.need(d)
        nc = self.nc
        with nc.Block() as block:
            for eng in self.engs:
                def body(e, eng=eng):
                    for f in eng.ops:
                        f(e)
                getattr(block, eng.attr)(body)


class MK:
    def __init__(self, phases, single=False):
        self.phases = phases
        self.single = single
        nc = bass.Bass("TRN2", target_bir_lowering=False)
        self.nc = nc
        self.P = Prog(nc)
        P = self.P
        self.xT_in = nc.dram_tensor("xT", [D, T], F32, kind="ExternalInput").ap()
        self.gains_in = nc.dram_tensor("gains", [128, 13 * KC], F32, kind="ExternalInput").ap()
        self.wgu_in = nc.dram_tensor("wgu", [8 * NJ, 128, KC * 256], F32, kind="ExternalInput").ap()
        self.wdn_in = nc.dram_tensor("wdn", [8 * NJ, 128, D], F32, kind="ExternalInput").ap()
        self.out_ap = nc.dram_tensor("out", [D, T], F32, kind="ExternalOutput").ap()
        self.wcv_in = nc.dram_tensor("wcv", [8, 128, KC * 640], F32, kind="ExternalInput").ap()
        self.wco_in = nc.dram_tensor("wco", [16, 128, D], F32, kind="ExternalInput").ap()
        self.cvp_in = nc.dram_tensor("cvp", [128, 8 * 37], F32, kind="ExternalInput").ap()
        self.wqkv_in = nc.dram_tensor("wqkv", [16, 128, KC * 384], F32, kind="ExternalInput").ap()
        self.wo_in = nc.dram_tensor("wo", [16, 128, D], F32, kind="ExternalInput").ap()
        self.tab_in = nc.dram_tensor("tab", [16, 128, 23 * 128], F32, kind="ExternalInput").ap()
        self.sel_in = nc.dram_tensor("sel", [128, 9], F32, kind="ExternalInput").ap()
        self.xch = {}
        for tag, ncols in (("c0", 32), ("c1", 32), ("a0", T), ("a1", T)):
            xin = nc.dram_tensor("xin_" + tag, [128, KC * ncols], BF16)
            xag = nc.dram_tensor("xag_" + tag, [NCORES * 128, KC * ncols], BF16)
            self.xch[tag] = (xin, xag, Buf("xin" + tag), Buf("xag" + tag))
        self.x = nc.alloc_sbuf_tensor("x", [128, KC, T], F32)
        self.xb = [[Buf(f"x{kc}_{tb}") for tb in range(NTB)] for kc in range(KC)]
        self.hT = nc.alloc_sbuf_tensor("hT", [128, KC, T], BF16)
        self.hb = [[Buf(f"h{kc}_{tb}") for tb in range(NTB)] for kc in range(KC)]
        self.gains = nc.alloc_sbuf_tensor("gains_sb", [128, 13 * KC], F32)
        self.gains_b = Buf("gains")
        self.ones = nc.alloc_sbuf_tensor("ones", [128, 128], BF16)
        self.ones_b = Buf("ones")
        self.eps_rms = nc.alloc_sbuf_tensor("eps_rms", [128, 1], F32)
        self.eps_ln = nc.alloc_sbuf_tensor("eps_ln", [128, 1], F32)
        GMAX = max(FF_GROUPS)
        self.AR = 45056
        self.arena = nc.alloc_sbuf_tensor("arena", [128, self.AR], BF16)
        ar = self.arena
        self.actT = ar[:, 0:GMAX * T].rearrange("p (g t) -> p g t", g=GMAX)
        self.actb = [[Buf(f"a{j}_{tb}") for tb in range(NTB)] for j in range(GMAX)]
        self.NDN = 12
        o = GMAX * T
        self.wdn = ar[:, o:o + self.NDN * D].rearrange("p (s d) -> p s d", s=self.NDN)
        self.wdnb = [Buf(f"wdn{i}") for i in range(self.NDN)]
        self.wdns = [P.dma_sem(f"d_wdn{i}") for i in range(self.NDN)]
        self.wdn_i = 0
        o += self.NDN * D
        self.NGU = 4
        self.wgu = ar[:, o:o + self.NGU * KC * 256].rearrange("p (s d) -> p s d", s=self.NGU)
        self.wgub = [Buf(f"wgu{i}") for i in range(self.NGU)]
        self.wgus = [P.dma_sem(f"d_wgu{i}") for i in range(self.NGU)]
        self.wgu_i = 0
        self.sq = nc.alloc_sbuf_tensor("sq", [128, 1, KC, TB], BF16)
        self.sqb = [[Buf(f"sq{i}_{kc}") for kc in range(KC)] for i in range(1)]
        self.rstd = nc.alloc_sbuf_tensor("rstd", [128, 2, TB], F32)
        self.rstdb = [Buf(f"rstd{i}") for i in range(2)]
        self.sg = nc.alloc_sbuf_tensor("sg", [128, 3, TB], F32)
        self.sgb = [Buf(f"sg{i}") for i in range(3)]
        self.sg_i = 0
        self.cvp = nc.alloc_sbuf_tensor("cvp_sb", [128, 8 * 37], F32)
        self.sel = nc.alloc_sbuf_tensor("sel_sb", [128, 9], F32)
        self.ones32 = nc.alloc_sbuf_tensor("ones32", [128, 128], F32)
        self.vmo = nc.alloc_sbuf_tensor("vmo", [128, 128], BF16)
        self.vmp = nc.alloc_sbuf_tensor("vmp", [128, 128], BF16)
        self.pen = nc.alloc_sbuf_tensor("pen", [128, 2], F32)
        self.pen_b = Buf("pen")
        self.cc_sem = P.dma_sem("cc")
        self.x_sem = P.dma_sem("d_xch")
        self.st_sems = [P.dma_sem("d_st0"), P.dma_sem("d_st1")]
        self.w1_sem = P.dma_sem("d_w1")
        self.w2_sem = P.dma_sem("d_w2")
        self.tab_sem = P.dma_sem("d_tab")
        self.ps = [nc.alloc_psum_tensor(f"ps{i}", [128, TB], F32) for i in range(8)]
        self.psb = [Buf(f"ps{i}") for i in range(8)]
        self.ps_i = 0
        self.misc_sem = P.dma_sem("d_misc")
        self.out_sem = P.dma_sem("d_out")
        self.build()

    def bank(self):
        i = self.ps_i
        self.ps_i = (i + 1) % 8
        return self.ps[i], self.psb[i]

    def load_consts(self):
        P = self.P
        x, xT_in = self.x, self.xT_in
        for kc in range(KC):
            P.dma(P.sp, lambda e, kc=kc: e.dma_start(out=x[:, kc, :], in_=xT_in[kc * 128:(kc + 1) * 128, :]),
                  self.misc_sem, writes=self.xb[kc], final=16 * (KC + 3))
        P.dma(P.sp, lambda e: e.dma_start(out=self.gains[:, :], in_=self.gains_in[:, :]),
              self.misc_sem, writes=[self.gains_b], final=16 * (KC + 3))
        NM = KC + 3
        P.dma(P.sp, lambda e: e.dma_start(out=self.cvp[:, :], in_=self.cvp_in[:, :]),
              self.misc_sem, writes=[Buf("cvp")], final=16 * NM)
        P.dma(P.sp, lambda e: e.dma_start(out=self.sel[:, :], in_=self.sel_in[:, :]),
              self.misc_sem, writes=[Buf("sel")], final=16 * NM)
        P.op(P.dve, lambda e: e.memset(self.ones32[:, :], 1.0 / 512), writes=[self.ones_b], inc=False)
        P.op(P.dve, lambda e: e.memset(self.vmo[:, :], 1.0), writes=[self.ones_b], inc=False)
        P.op(P.dve, lambda e: e.memset(self.ones[:, :], 1.0 / D), writes=[self.ones_b], inc=False)
        P.op(P.dve, lambda e: e.memset(self.eps_rms[:, :], RMS_EPS), writes=[self.ones_b], inc=False)
        P.op(P.dve, lambda e: e.memset(self.eps_ln[:, :], LN_EPS), writes=[self.ones_b])

    def rmsnorm(self, nidx):
        P = self.P
        x, hT = self.x, self.hT
        for tb in range(NTB):
            ts = slice(tb * TB, (tb + 1) * TB)
            si = tb % 2
            for kc in range(KC):
                P.op(P.act, lambda e, kc=kc, ts=ts, si=si: e.activation(
                    out=self.sq[:, 0, kc, :], in_=x[:, kc, ts], func=AF.Square),
                    reads=[self.xb[kc][tb]], writes=[self.sqb[0][kc]])
            ps, psb = self.bank()
            for kc in range(KC):
                P.op(P.pe, lambda e, kc=kc, si=si, ps=ps: e.matmul(
                    ps[:, :], lhsT=self.ones[:, :], rhs=self.sq[:, 0, kc, :], start=(kc == 0), stop=(kc == KC - 1)),
                    reads=[self.ones_b, self.sqb[0][kc]], writes=[psb], inc=(kc == KC - 1))
            P.op(P.act, lambda e, si=si, ps=ps: e.activation(
                out=self.rstd[:, si, :], in_=ps[:, :], func=AF.Sqrt, bias=self.eps_rms[:, :], scale=1.0),
                reads=[psb, self.ones_b], writes=[self.rstdb[si]])
            P.op(P.dve, lambda e, si=si: e.reciprocal(out=self.rstd[:, si, :], in_=self.rstd[:, si, :]),
                reads=[self.rstdb[si]], writes=[self.rstdb[si]])
            for kc in range(KC):
                g = self.gains[:, nidx * KC + kc: nidx * KC + kc + 1]
                P.op(P.dve, lambda e, kc=kc, ts=ts, si=si, g=g: e.scalar_tensor_tensor(
                    out=hT[:, kc, ts], in0=x[:, kc, ts], scalar=g, in1=self.rstd[:, si, :],
                    op0=ALU.mult, op1=ALU.mult),
                    reads=[self.xb[kc][tb], self.rstdb[si], self.gains_b], writes=[self.hb[kc][tb]])

    def ffn(self, fidx, nidx):
        P = self.P
        x, hT, actT = self.x, self.hT, self.actT
        self.rmsnorm(nidx)
        j0 = 0
        for G in FF_GROUPS:
            gu_slots = []
            for jj in range(G):
                s = self.wgu_i
                self.wgu_i = (s + 1) % self.NGU
                gu_slots.append(s)
            dn_slots = []
            for jj in range(G):
                s = self.wdn_i
                self.wdn_i = (s + 1) % self.NDN
                dn_slots.append(s)
            for jj in range(G):
                s = gu_slots[jj]
                src = self.wgu_in[fidx * NJ + j0 + jj]
                P.dma(P.pool, lambda e, s=s, src=src: e.dma_start(out=self.wgu[:, s, :], in_=src),
                      self.wgus[s], writes=[self.wgub[s]])
                for tb in range(NTB):
                    ts = slice(tb * TB, (tb + 1) * TB)
                    pg, pgb = self.bank()
                    pu, pub = self.bank()
                    for half, (pp, ppb) in enumerate(((pg, pgb), (pu, pub))):
                        for kc in range(KC):
                            w = self.wgu[:, s, kc * 256 + half * 128: kc * 256 + half * 128 + 128]
                            P.op(P.pe, lambda e, pp=pp, w=w, kc=kc, ts=ts: e.matmul(
                                pp[:, :], lhsT=w, rhs=hT[:, kc, ts], start=(kc == 0), stop=(kc == KC - 1)),
                                reads=[self.wgub[s], self.hb[kc][tb]], writes=[ppb], inc=(kc == KC - 1))
                    si = self.sg_i
                    self.sg_i = (si + 1) % 3
                    P.op(P.act, lambda e, pg=pg, si=si: e.activation(out=self.sg[:, si, :], in_=pg[:, :], func=AF.Silu),
                         reads=[pgb], writes=[self.sgb[si]])
                    P.op(P.dve, lambda e, pu=pu, si=si, jj=jj, ts=ts: e.tensor_tensor(
                        out=actT[:, jj, ts], in0=self.sg[:, si, :], in1=pu[:, :], op=ALU.mult),
                        reads=[self.sgb[si], pub], writes=[self.actb[jj][tb]])
            for jj in range(G):
                s = dn_slots[jj]
                src = self.wdn_in[fidx * NJ + j0 + jj]
                P.dma(P.pool, lambda e, s=s, src=src: e.dma_start(out=self.wdn[:, s, :], in_=src),
                      self.wdns[s], writes=[self.wdnb[s]])
            for c in range(KC):
                for tb in range(NTB):
                    ts = slice(tb * TB, (tb + 1) * TB)
                    po, pob = self.bank()
                    for jj in range(G):
                        s = dn_slots[jj]
                        P.op(P.pe, lambda e, po=po, s=s, c=c, jj=jj, ts=ts, G=G: e.matmul(
                            po[:, :], lhsT=self.wdn[:, s, c * 128:(c + 1) * 128], rhs=actT[:, jj, ts],
                            start=(jj == 0), stop=(jj == G - 1)),
                            reads=[self.wdnb[s], self.actb[jj][tb]], writes=[pob], inc=(jj == G - 1))
                    P.op(P.dve, lambda e, po=po, c=c, ts=ts: e.scalar_tensor_tensor(
                        out=x[:, c, ts], in0=po[:, :], scalar=0.5, in1=x[:, c, ts], op0=ALU.mult, op1=ALU.add),
                        reads=[pob, self.xb[c][tb]], writes=[self.xb[c][tb]])
            j0 += G

    def exchange(self, tag, ncols, prev, prev_b):
        P = self.P
        xin, xag, xin_b, xag_b = self.xch[tag]
        if self.single:
            P.op(P.dve, lambda e: e.memset(prev, 0.0), writes=[prev_b])
            return
        hb_all = [b for row in self.hb for b in row]
        P.dma(P.sp, lambda e: e.dma_start(out=xin.ap().rearrange("p (k t) -> p k t", k=KC), in_=self.hT[:, :, T - ncols:T]),
              self.x_sem, reads=hb_all, writes=[xin_b])
        P.coll(lambda e: e.collective_compute("AllGather", ALU.bypass, replica_groups=[list(range(NCORES))],
                                              ins=[xin.ap().opt()], outs=[xag.ap().opt()]),
               self.cc_sem, reads=[xin_b], writes=[xag_b])
        i = 0
        for kc in range(KC):
            for r in (0, 1, 2, 4, 5, 6):
                st = self.stage[i % 2]
                stb = self.stage_b[i % 2]
                src = xag.ap()[r * 128:(r + 1) * 128, kc * ncols:(kc + 1) * ncols]
                P.dma(P.sp, lambda e, st=st, src=src: e.dma_start(out=st[:, 0:ncols], in_=src),
                      self.st_sems[i % 2], reads=[xag_b], writes=[stb])
                if r == 0:
                    P.op(P.dve, lambda e, st=st, kc=kc, r=r: e.tensor_scalar(
                        out=prev[:, kc, :], in0=st[:, 0:ncols], scalar1=self.sel[:, r:r + 1], scalar2=None, op0=ALU.mult),
                        reads=[stb, self.gains_b], writes=[prev_b])
                else:
                    P.op(P.dve, lambda e, st=st, kc=kc, r=r: e.scalar_tensor_tensor(
                        out=prev[:, kc, :], in0=st[:, 0:ncols], scalar=self.sel[:, r:r + 1], in1=prev[:, kc, :],
                        op0=ALU.mult, op1=ALU.add),
                        reads=[stb, self.gains_b], writes=[prev_b])
                i += 1

    def proj_residual(self, w, wb, rhs, rhs_b, scale=1.0):
        P = self.P
        x = self.x
        for c in range(KC):
            for tb in range(NTB):
                ts = slice(tb * TB, (tb + 1) * TB)
                po, pob = self.bank()
                P.op(P.pe, lambda e, po=po, c=c, ts=ts: e.matmul(
                    po[:, :], lhsT=w[:, c * 128:(c + 1) * 128], rhs=rhs[:, ts], start=True, stop=True),
                    reads=[wb, rhs_b], writes=[pob])
                P.op(P.dve, lambda e, po=po, c=c, ts=ts: e.scalar_tensor_tensor(
                    out=x[:, c, ts], in0=po[:, :], scalar=scale, in1=x[:, c, ts], op0=ALU.mult, op1=ALU.add),
                    reads=[pob, self.xb[c][tb]], writes=[self.xb[c][tb]])

    def conv(self, ci, nidx):
        P = self.P
        ar = self.arena
        hT = self.hT
        self.rmsnorm(nidx)
        o = 0
        def take(n):
            nonlocal o
            v = ar[:, o:o + n]
            o += n
            return v
        wcv = take(KC * 640).rearrange("p (k n) -> p k n", k=KC); wcv_b = Buf("wcv")
        cx = take(2 * 2056).bitcast(F32); cx_b = Buf("cx")
        u = take(2 * 2080).bitcast(F32); u_b = Buf("u")
        tmp = take(2 * 2 * TB).bitcast(F32).rearrange("p (i t) -> p i t", i=2); tmp_b = [Buf("tmp0"), Buf("tmp1")]
        v = take(2 * 4 * T).bitcast(F32).rearrange("p (q t) -> p q t", q=4); v_b = [[Buf(f"v{q}_{tb}") for tb in range(NTB)] for q in range(4)]
        ya = take(T); ya_b = Buf("ya")
        wout = take(2 * D).rearrange("p (i d) -> p i d", i=2); wout_b = [Buf("wo0"), Buf("wo1")]
        hprev = take(KC * 32).rearrange("p (k t) -> p k t", k=KC); hprev_b = Buf("hprev")
        self.stage = [take(KC * 32), take(KC * 32)]; self.stage_b = [Buf("st0"), Buf("st1")]
        tmp4 = take(2 * 4 * TB).bitcast(F32).rearrange("p (q t) -> p q t", q=4); tmp4_b = [Buf(f"t4{q}") for q in range(4)]
        yb = take(4 * TB).rearrange("p (q t) -> p q t", q=4); yb_b = [Buf(f"yb{q}") for q in range(4)]
        acc = take(2 * TB).bitcast(F32); acc_b = Buf("acc")
        assert o <= self.AR
        self.exchange(f"c{ci}", 32, hprev, hprev_b)
        wo_i = 0
        for q in range(4):
            pc = (ci * 4 + q) * 37
            cp = lambda j, pc=pc: self.cvp[:, pc + j:pc + j + 1]
            P.dma(P.pool, lambda e, q=q: e.dma_start(out=wcv, in_=self.wcv_in[ci * 4 + q].rearrange("p (k n) -> p k n", k=KC)),
                  self.w1_sem, writes=[wcv_b])
            ph, phb = self.bank()
            for g in range(5):
                for kc in range(KC):
                    P.op(P.pe, lambda e, g=g, kc=kc, ph=ph: e.matmul(
                        ph[:, g * 32:(g + 1) * 32], lhsT=wcv[:, kc, g * 128:(g + 1) * 128], rhs=hprev[:, kc, :],
                        start=(kc == 0), stop=(kc == KC - 1)),
                        reads=[wcv_b, hprev_b], writes=[phb], inc=(kc == KC - 1))
            P.op(P.act, lambda e, ph=ph: e.activation(out=tmp[:, 0, 0:32], in_=ph[:, 32:64], func=AF.Copy),
                 reads=[phb], writes=[tmp_b[0]])
            P.op(P.dve, lambda e, ph=ph: e.tensor_tensor(out=cx[:, 0:2], in0=tmp[:, 0, 30:32], in1=ph[:, 94:96], op=ALU.mult),
                 reads=[phb, tmp_b[0]], writes=[cx_b])
            P.op(P.act, lambda e, ph=ph: e.activation(out=tmp[:, 1, 0:32], in_=ph[:, 128:160], func=AF.Sigmoid),
                 reads=[phb], writes=[tmp_b[1]])
            P.op(P.dve, lambda e, ph=ph: e.tensor_tensor(out=u[:, 0:30], in0=tmp[:, 1, 2:32], in1=ph[:, 98:128], op=ALU.mult),
                 reads=[phb, tmp_b[1]], writes=[u_b])
            for tb in range(NTB):
                ts = slice(tb * TB, (tb + 1) * TB)
                pz = []
                for g in range(5):
                    pp, ppb = self.bank()
                    pz.append((pp, ppb))
                    for kc in range(KC):
                        P.op(P.pe, lambda e, g=g, kc=kc, pp=pp, ts=ts: e.matmul(
                            pp[:, :], lhsT=wcv[:, kc, g * 128:(g + 1) * 128], rhs=hT[:, kc, ts],
                            start=(kc == 0), stop=(kc == KC - 1)),
                            reads=[wcv_b, self.hb[kc][tb]], writes=[ppb], inc=(kc == KC - 1))
                (pab, pabb), (pac, pacb), (pax, paxb), (pbv, pbvb), (pbg, pbgb) = pz
                P.op(P.act, lambda e, pac=pac: e.activation(out=tmp[:, 0, :], in_=pac[:, :], func=AF.Copy),
                     reads=[pacb], writes=[tmp_b[0]])
                P.op(P.dve, lambda e, pax=pax, tb=tb: e.tensor_tensor(
                    out=cx[:, 2 + tb * TB:2 + (tb + 1) * TB], in0=tmp[:, 0, :], in1=pax[:, :], op=ALU.mult),
                    reads=[paxb, tmp_b[0]], writes=[cx_b])
                P.op(P.dve, lambda e, tb=tb, cp=cp: e.tensor_scalar(
                    out=acc[:, :], in0=cx[:, tb * TB:tb * TB + TB], scalar1=cp(0), scalar2=None, op0=ALU.mult),
                    reads=[cx_b, self.gains_b], writes=[acc_b])
                for k in (1, 2):
                    P.op(P.dve, lambda e, tb=tb, k=k, cp=cp: e.scalar_tensor_tensor(
                        out=acc[:, :], in0=cx[:, tb * TB + k:tb * TB + k + TB], scalar=cp(k), in1=acc[:, :],
                        op0=ALU.mult, op1=ALU.add),
                        reads=[cx_b, acc_b], writes=[acc_b])
                P.op(P.dve, lambda e, pab=pab, ts=ts: e.tensor_tensor(out=ya[:, ts], in0=acc[:, :], in1=pab[:, :], op=ALU.mult),
                     reads=[acc_b, pabb], writes=[ya_b])
                P.op(P.act, lambda e, pbg=pbg: e.activation(out=tmp[:, 1, :], in_=pbg[:, :], func=AF.Sigmoid),
                     reads=[pbgb], writes=[tmp_b[1]])
                P.op(P.dve, lambda e, pbv=pbv, tb=tb: e.tensor_tensor(
                    out=u[:, 30 + tb * TB:30 + (tb + 1) * TB], in0=tmp[:, 1, :], in1=pbv[:, :], op=ALU.mult),
                    reads=[pbvb, tmp_b[1]], writes=[u_b])
                P.op(P.dve, lambda e, tb=tb, q=q, ts=ts, cp=cp: e.tensor_scalar(
                    out=v[:, q, ts], in0=u[:, tb * TB:tb * TB + TB], scalar1=cp(3), scalar2=cp(34), op0=ALU.mult, op1=ALU.add),
                    reads=[u_b, self.gains_b], writes=[v_b[q][tb]])
                for k in range(1, 31):
                    P.op(P.dve, lambda e, tb=tb, k=k, q=q, ts=ts, cp=cp: e.scalar_tensor_tensor(
                        out=v[:, q, ts], in0=u[:, tb * TB + k:tb * TB + k + TB], scalar=cp(3 + k), in1=v[:, q, ts],
                        op0=ALU.mult, op1=ALU.add),
                        reads=[u_b, v_b[q][tb]], writes=[v_b[q][tb]])
            wi = 0
            wo_i += 1
            P.dma(P.pool, lambda e, wi=wi, q=q: e.dma_start(out=wout[:, wi, :], in_=self.wco_in[ci * 8 + q]),
                  self.w2_sem, writes=[wout_b[wi]])
            self.proj_residual(wout[:, wi, :], wout_b[wi], ya, ya_b)
        wv = ar[:, 0:4 * D].rearrange("p (q d) -> p q d", q=4)
        wv_b = wcv_b
        P.dma(P.pool, lambda e: e.dma_start(out=wv, in_=self.wco_in[ci * 8 + 4:ci * 8 + 8].rearrange("q p d -> p q d")),
              self.w1_sem, writes=[wv_b])
        for tb in range(NTB):
            ts = slice(tb * TB, (tb + 1) * TB)
            pm, pmb = self.bank()
            for q in range(4):
                P.op(P.pe, lambda e, q=q, pm=pm, ts=ts: e.matmul(
                    pm[:, :], lhsT=self.ones32[:, :], rhs=v[:, q, ts], start=(q == 0), stop=(q == 3)),
                    reads=[self.ones_b, v_b[q][tb]], writes=[pmb], inc=(q == 3))
            for q in range(4):
                P.op(P.act, lambda e, q=q, ts=ts: e.activation(out=tmp4[:, q, :], in_=v[:, q, ts], func=AF.Square),
                     reads=[v_b[q][tb]], writes=[tmp4_b[q]])
            pq, pqb = self.bank()
            for q in range(4):
                P.op(P.pe, lambda e, q=q, pq=pq: e.matmul(
                    pq[:, :], lhsT=self.ones32[:, :], rhs=tmp4[:, q, :], start=(q == 0), stop=(q == 3)),
                    reads=[self.ones_b, tmp4_b[q]], writes=[pqb], inc=(q == 3))
            P.op(P.act, lambda e, pm=pm: e.activation(out=tmp[:, 0, :], in_=pm[:, :], func=AF.Square),
                 reads=[pmb], writes=[tmp_b[0]])
            P.op(P.dve, lambda e, pq=pq: e.tensor_tensor(out=tmp[:, 0, :], in0=pq[:, :], in1=tmp[:, 0, :], op=ALU.subtract),
                 reads=[pqb, tmp_b[0]], writes=[tmp_b[0]])
            P.op(P.act, lambda e: e.activation(out=tmp[:, 0, :], in_=tmp[:, 0, :], func=AF.Sqrt, bias=self.eps_ln[:, :], scale=1.0),
                 reads=[tmp_b[0], self.ones_b], writes=[tmp_b[0]])
            P.op(P.dve, lambda e: e.reciprocal(out=tmp[:, 0, :], in_=tmp[:, 0, :]), reads=[tmp_b[0]], writes=[tmp_b[0]])
            for q in range(4):
                pc = (ci * 4 + q) * 37
                P.op(P.dve, lambda e, q=q, ts=ts, pm=pm: e.tensor_tensor(out=tmp4[:, q, :], in0=v[:, q, ts], in1=pm[:, :], op=ALU.subtract),
                     reads=[v_b[q][tb], pmb], writes=[tmp4_b[q]])
                P.op(P.dve, lambda e, q=q: e.tensor_tensor(out=tmp4[:, q, :], in0=tmp4[:, q, :], in1=tmp[:, 0, :], op=ALU.mult),
                     reads=[tmp4_b[q], tmp_b[0]], writes=[tmp4_b[q]])
                P.op(P.act, lambda e, q=q, pc=pc: e.activation(out=yb[:, q, :], in_=tmp4[:, q, :], func=AF.Silu,
                                                            bias=self.cvp[:, pc + 36:pc + 37], scale=self.cvp[:, pc + 35:pc + 36]),
                     reads=[tmp4_b[q], self.gains_b], writes=[yb_b[q]])
            for c in range(KC):
                po, pob = self.bank()
                for q in range(4):
                    P.op(P.pe, lambda e, po=po, c=c, q=q: e.matmul(
                        po[:, :], lhsT=wv[:, q, c * 128:(c + 1) * 128], rhs=yb[:, q, :], start=(q == 0), stop=(q == 3)),
                        reads=[wv_b, yb_b[q]], writes=[pob], inc=(q == 3))
                P.op(P.dve, lambda e, po=po, c=c, ts=ts: e.tensor_tensor(out=self.x[:, c, ts], in0=po[:, :], in1=self.x[:, c, ts], op=ALU.add),
                     reads=[pob, self.xb[c][tb]], writes=[self.xb[c][tb]])

    def attn(self, ai, nidx):
        P = self.P
        ar = self.arena
        hT = self.hT
        self.rmsnorm(nidx)
        o = 0
        def take(n):
            nonlocal o
            v = ar[:, o:o + n]
            o += n
            return v
        hprev = take(KC * T).rearrange("p (k t) -> p k t", k=KC); hprev_b = Buf("hprev")
        Qm = take(T); Qm_b = Buf("Qm")
        Km = take(2 * T); Km_b = Buf("Km")
        Vm = take(32 * 128).rearrange("p (k d) -> p k d", k=32); Vm_b = Buf("Vm")
        tab = take(2 * 23 * 128).bitcast(F32); tab_b = Buf("tab")
        OTm = take(T); OTm_b = Buf("OTm")
        wqkv = take(KC * 384).rearrange("p (k n) -> p k n", k=KC); wqkv_b = Buf("wqkv")
        wo = take(D); wo_b = Buf("wo")
        o_alias = o
        NS = 3
        sT = take(2 * NS * TB).bitcast(F32).rearrange("p (i t) -> p i t", i=NS); sT_b = [Buf(f"sT{i}") for i in range(NS)]
        pT = take(NS * TB).rearrange("p (i t) -> p i t", i=NS); pT_b = [Buf(f"pT{i}") for i in range(NS)]
        rdn = take(2 * TB).bitcast(F32); rdn_b = Buf("rdn")
        assert o <= self.AR, o
        assert o - o_alias >= 2 * T
        self.stage = [ar[:, o_alias:o_alias + T], ar[:, o_alias + T:o_alias + 2 * T]]; self.stage_b = [Buf("st0"), Buf("st1")]
        self.exchange(f"a{ai}", T, hprev, hprev_b)
        for bb in sT_b + pT_b + [rdn_b]:
            bb.w = hprev_b.w
            bb.r = list(self.stage_b[0].r) + list(self.stage_b[1].r)
        P.op(P.dve, lambda e: e.tensor_scalar(out=self.pen[:, 0:1], in0=self.sel[:, 8:9], scalar1=30000.0, scalar2=-30000.0,
                                               op0=ALU.mult, op1=ALU.add),
             reads=[self.gains_b], writes=[self.pen_b], inc=False)
        P.op(P.dve, lambda e: e.memset(self.pen[:, 1:2], 0.0), writes=[self.pen_b])
        SB = [0, 1, 2, 3]
        UB = [4, 5]
        DB = [6, 7]
        for m in range(8):
            P.dma(P.pool, lambda e, m=m: e.dma_start(out=wqkv, in_=self.wqkv_in[ai * 8 + m].rearrange("p (k n) -> p k n", k=KC)),
                  self.w1_sem, writes=[wqkv_b])
            P.dma(P.pool, lambda e, m=m: e.dma_start(out=wo, in_=self.wo_in[ai * 8 + m]), self.w2_sem, writes=[wo_b])
            for tb in range(NTB):
                ts = slice(tb * TB, (tb + 1) * TB)
                pp, ppb = self.bank()
                for kc in range(KC):
                    P.op(P.pe, lambda e, kc=kc, pp=pp, ts=ts: e.matmul(
                        pp[:, :], lhsT=wqkv[:, kc, 0:128], rhs=hT[:, kc, ts], start=(kc == 0), stop=(kc == KC - 1)),
                        reads=[wqkv_b, self.hb[kc][tb]], writes=[ppb], inc=(kc == KC - 1))
                P.op(P.act, lambda e, pp=pp, ts=ts: e.activation(out=Qm[:, ts], in_=pp[:, :], func=AF.Copy),
                     reads=[ppb], writes=[Qm_b])
            for eb in range(8):
                pp, ppb = self.bank()
                for kc in range(KC):
                    if eb < 4:
                        rhs = hprev[:, kc, eb * TB:(eb + 1) * TB]; rb = hprev_b
                    else:
                        rhs = hT[:, kc, (eb - 4) * TB:(eb - 3) * TB]; rb = self.hb[kc][eb - 4]
                    P.op(P.pe, lambda e, kc=kc, pp=pp, rhs=rhs: e.matmul(
                        pp[:, :], lhsT=wqkv[:, kc, 128:256], rhs=rhs, start=(kc == 0), stop=(kc == KC - 1)),
                        reads=[wqkv_b, rb], writes=[ppb], inc=(kc == KC - 1))
                P.op(P.dve, lambda e, pp=pp, eb=eb: e.tensor_copy(out=Km[:, eb * TB:(eb + 1) * TB], in_=pp[:, :]),
                     reads=[ppb], writes=[Km_b])
            for k4 in range(8):
                pp, ppb = self.bank()
                for j in range(4):
                    kb = k4 * 4 + j
                    for kc in range(KC):
                        if kb < 16:
                            lh = hprev[:, kc, kb * 128:(kb + 1) * 128]; rb = hprev_b
                        else:
                            lh = hT[:, kc, (kb - 16) * 128:(kb - 15) * 128]; rb = self.hb[kc][(kb - 16) // 4]
                        P.op(P.pe, lambda e, kc=kc, pp=pp, lh=lh, j=j: e.matmul(
                            pp[:, j * 128:(j + 1) * 128], lhsT=lh, rhs=wqkv[:, kc, 256:384], start=(kc == 0), stop=(kc == KC - 1)),
                            reads=[wqkv_b, rb], writes=[ppb], inc=(kc == KC - 1))
                P.op(P.act, lambda e, pp=pp, k4=k4: e.activation(
                    out=Vm[:, k4 * 4:(k4 + 1) * 4, :], in_=pp[:, :].rearrange("p (j d) -> p j d", j=4), func=AF.Copy),
                    reads=[ppb], writes=[Vm_b])
            steps = [(hh, g, kbe) for hh in range(2) for g in range(4) for kbe in range(4 * g, 4 * g + 20)]
            LOOK = 2
            sbank = {}
            def emit_S(i):
                hh, g, kbe = steps[i]
                hs = slice(hh * 64, (hh + 1) * 64)
                bi = SB[i % 4]
                sbank[i] = bi
                P.op(P.pe, lambda e, bi=bi, hs=hs, kbe=kbe, g=g: e.matmul(
                    self.ps[bi][:, :], lhsT=Km[hs, kbe * 128:(kbe + 1) * 128], rhs=Qm[hs, g * TB:(g + 1) * TB], start=True, stop=True),
                    reads=[Km_b, Qm_b], writes=[self.psb[bi]])
            def emit_rest(i):
                hh, g, kbe = steps[i]
                hs = slice(hh * 64, (hh + 1) * 64)
                bi = sbank[i]
                si = i % NS
                r0 = 4 * g - kbe + 16 + 3
                u = (hh * 4 + g) % 2
                pu, pub = self.ps[UB[u]], self.psb[UB[u]]
                pd, pdb = self.ps[DB[u]], self.psb[DB[u]]
                first, last = (kbe == 4 * g), (kbe == 4 * g + 19)
                if first and g == 0:
                    head = 2 * m + hh
                    P.dma(P.sp, lambda e, head=head: e.dma_start(out=tab, in_=self.tab_in[head]), self.tab_sem, writes=[tab_b])
                P.op(P.dve, lambda e, bi=bi, si=si, r0=r0: e.scalar_tensor_tensor(
                    out=sT[:, si, :], in0=self.ps[bi][:, :], scalar=0.125, in1=tab[:, r0 * 128:(r0 + 4) * 128],
                    op0=ALU.mult, op1=ALU.add),
                    reads=[self.psb[bi], tab_b], writes=[sT_b[si]])
                pcol = 0 if kbe < 16 else 1
                P.op(P.act, lambda e, si=si, pcol=pcol: e.activation(out=pT[:, si, :], in_=sT[:, si, :], func=AF.Exp,
                                                                    bias=self.pen[:, pcol:pcol + 1], scale=1.0),
                     reads=[sT_b[si], self.pen_b], writes=[pT_b[si]])
                P.op(P.pe, lambda e, pu=pu, kbe=kbe, si=si, first=first, last=last: e.matmul(
                    pu[:, :], lhsT=Vm[:, kbe, :], rhs=pT[:, si, :], start=first, stop=last),
                    reads=[Vm_b, pT_b[si]], writes=[pub], inc=False)
                P.op(P.pe, lambda e, pd=pd, si=si, first=first, last=last: e.matmul(
                    pd[:, :], lhsT=self.vmo[:, :], rhs=pT[:, si, :], start=first, stop=last),
                    reads=[self.ones_b, pT_b[si]], writes=[pdb])
                if last:
                    gs = slice(g * TB, (g + 1) * TB)
                    P.op(P.act, lambda e, pd=pd, hs=hs: e.activation(out=rdn[hs, :], in_=pd[hs, :], func=AF.Copy),
                         reads=[pdb], writes=[rdn_b])
                    P.op(P.dve, lambda e, hs=hs: e.reciprocal(out=rdn[hs, :], in_=rdn[hs, :]), reads=[rdn_b], writes=[rdn_b])
                    P.op(P.dve, lambda e, pu=pu, hs=hs, gs=gs: e.tensor_tensor(out=OTm[hs, gs], in0=pu[hs, :], in1=rdn[hs, :], op=ALU.mult),
                         reads=[pub, rdn_b], writes=[OTm_b])
            for i in range(min(LOOK, len(steps))):
                emit_S(i)
            for i in range(len(steps)):
                if i + LOOK < len(steps):
                    emit_S(i + LOOK)
                emit_rest(i)
            self.proj_residual(wo, wo_b, OTm, OTm_b)

    def final_norm(self):
        P = self.P
        x = self.x
        nidx = 12
        for tb in range(NTB):
            ts = slice(tb * TB, (tb + 1) * TB)
            si = tb % 2
            for kc in range(KC):
                P.op(P.act, lambda e, kc=kc, ts=ts, si=si: e.activation(
                    out=self.sq[:, 0, kc, :], in_=x[:, kc, ts], func=AF.Square),
                    reads=[self.xb[kc][tb]], writes=[self.sqb[0][kc]])
            ps, psb = self.bank()
            for kc in range(KC):
                P.op(P.pe, lambda e, kc=kc, si=si, ps=ps: e.matmul(
                    ps[:, :], lhsT=self.ones[:, :], rhs=self.sq[:, 0, kc, :], start=(kc == 0), stop=(kc == KC - 1)),
                    reads=[self.ones_b, self.sqb[0][kc]], writes=[psb], inc=(kc == KC - 1))
            P.op(P.act, lambda e, si=si, ps=ps: e.activation(
                out=self.rstd[:, si, :], in_=ps[:, :], func=AF.Sqrt, bias=self.eps_rms[:, :], scale=1.0),
                reads=[psb, self.ones_b], writes=[self.rstdb[si]])
            P.op(P.dve, lambda e, si=si: e.reciprocal(out=self.rstd[:, si, :], in_=self.rstd[:, si, :]),
                reads=[self.rstdb[si]], writes=[self.rstdb[si]])
            for kc in range(KC):
                g = self.gains[:, nidx * KC + kc: nidx * KC + kc + 1]
                P.op(P.dve, lambda e, kc=kc, ts=ts, si=si, g=g: e.scalar_tensor_tensor(
                    out=x[:, kc, ts], in0=x[:, kc, ts], scalar=g, in1=self.rstd[:, si, :],
                    op0=ALU.mult, op1=ALU.mult),
                    reads=[self.xb[kc][tb], self.rstdb[si], self.gains_b], writes=[self.xb[kc][tb]])

    def store_x(self):
        P = self.P
        for kc in range(KC):
            P.dma(P.sp, lambda e, kc=kc: e.dma_start(out=self.out_ap[kc * 128:(kc + 1) * 128, :], in_=self.x[:, kc, :]),
                  self.out_sem, reads=self.xb[kc], final=16 * KC)

    def build(self):
        P = self.P
        self.load_consts()
        for ph in self.phases:
            kind = ph[0]
            if kind == "ffn":
                self.ffn(ph[1], ph[2])
            elif kind == "final":
                self.final_norm()
            elif kind == "conv":
                self.conv(ph[1], ph[2])
            elif kind == "attn":
                self.attn(ph[1], ph[2])
            P.barrier(P.engs)
        self.store_x()
        P.finish([(self.out_sem, self.out_sem.count)])


def prep_weights(inp):
    f = lambda k: np.asarray(inp[k], np.float32)
    gains = np.stack([f("ffn1_norm")[l] for l in range(4)] + [f("mix_norm")[l] for l in range(4)]
                     + [f("ffn2_norm")[l] for l in range(4)] + [f("final_norm")], 0)
    gains = np.ascontiguousarray(gains.reshape(13, KC, 128).transpose(2, 0, 1).reshape(128, 13 * KC))
    wgu = np.empty((8, NJ, 128, KC, 256), np.float32)
    wdn = np.empty((8, NJ, 128, D), np.float32)
    for l in range(4):
        for w, (kgu, kdn) in enumerate((("ffn1_w_gate_up", "ffn1_w_down"), ("ffn2_w_gate_up", "ffn2_w_down"))):
            fi = l * 2 + w
            gu = f(kgu)[l]
            g = gu[:, :DFF].reshape(KC, 128, NJ, 128)
            u = gu[:, DFF:].reshape(KC, 128, NJ, 128)
            wgu[fi, :, :, :, :128] = g.transpose(2, 1, 0, 3)
            wgu[fi, :, :, :, 128:] = u.transpose(2, 1, 0, 3)
            wdn[fi] = f(kdn)[l].reshape(NJ, 128, D)
    wcv = np.empty((2, 4, 128, KC, 5, 128), np.float32)
    wco = np.empty((2, 8, 128, D), np.float32)
    cvp = np.empty((128, 2, 4, 37), np.float32)
    for ci in range(2):
        win = f("conv_w_in")[ci].reshape(KC, 128, 5, 4, 128)
        wcv[ci] = win.transpose(3, 1, 0, 2, 4)
        wco[ci] = f("conv_w_out")[ci].reshape(8, 128, D)
        ak = f("conv_a_kernel")[ci].reshape(3, 4, 128)
        bk = f("conv_b_kernel")[ci].reshape(31, 4, 128)
        cvp[:, ci, :, 0:3] = ak.transpose(2, 1, 0)
        cvp[:, ci, :, 3:34] = bk.transpose(2, 1, 0)
        cvp[:, ci, :, 34] = f("conv_b_bias")[ci].reshape(4, 128).T
        cvp[:, ci, :, 35] = f("conv_b_ln_gain")[ci].reshape(4, 128).T
        cvp[:, ci, :, 36] = f("conv_b_ln_bias")[ci].reshape(4, 128).T
    wqkv = np.empty((2, 8, 128, KC, 3, 128), np.float32)
    wo = np.empty((2, 8, 128, D), np.float32)
    for ai in range(2):
        w = f("attn_w_qkv")[ai].reshape(KC, 128, 3, 8, 128)
        wqkv[ai] = w.transpose(3, 1, 0, 2, 4)
        wo[ai] = f("attn_w_o")[ai].reshape(8, 128, D)
    k = np.arange(128)[:, None, None]
    j = np.arange(23)[None, :, None]
    q = np.arange(128)[None, None, :]
    dlt = (j - 3) * 128 + q - k
    mult = ((dlt <= 128).astype(np.float64) + ((dlt % 4 == 0) & (dlt <= 512)) + ((dlt % 16 == 0) & (dlt <= 2048)))
    ok = (dlt >= 0) & (mult > 0)
    lnm = np.log(np.where(ok, mult, 1.0))
    tab = np.empty((16, 128, 23 * 128), np.float32)
    for h in range(16):
        slope = 2.0 ** (-8.0 * (h + 1) / 16)
        tab[h] = np.where(ok, -slope * dlt + lnm, -30000.0).reshape(128, 23 * 128)
    return dict(gains=gains, wgu=wgu.reshape(8 * NJ, 128, KC * 256), wdn=wdn.reshape(8 * NJ, 128, D),
                wcv=wcv.reshape(8, 128, KC * 640), wco=wco.reshape(16, 128, D), cvp=cvp.reshape(128, 8 * 37),
                wqkv=wqkv.reshape(16, 128, KC * 384), wo=wo.reshape(16, 128, D), tab=tab)


def ffn_idx(layer, which):
    return layer * 2 + which


def full_phases():
    ph = []
    for l in range(DEPTH):
        ph.append(("ffn", ffn_idx(l, 0), l))
        if l % 2 == 0:
            ph.append(("conv", l // 2, 4 + l))
        else:
            ph.append(("attn", l // 2, 4 + l))
        ph.append(("ffn", ffn_idx(l, 1), 8 + l))
    ph.append(("final",))
    return ph


def make_in_maps(inp):
    x = np.asarray(inp["x"], np.float32)
    W = prep_weights(inp)
    in_maps = []
    for c in range(NCORES):
        b, ch = divmod(c, 4)
        xT = np.ascontiguousarray(x[b, ch * T:(ch + 1) * T, :].T)
        sel = np.zeros((128, 9), np.float32)
        if ch > 0:
            sel[:, c - 1] = 1.0
            sel[:, 8] = 1.0
        m = dict(W)
        m["xT"] = xT
        m["sel"] = sel
        in_maps.append(m)
    return in_maps


def run_phases(inp, phases):
    x = np.asarray(inp["x"], np.float32)
    mk = MK(phases)
    in_maps = make_in_maps(inp)
    res = run_bass_kernel_spmd(mk.nc, in_maps, core_ids=list(range(NCORES)))
    out = np.empty_like(x)
    for c in range(NCORES):
        b, ch = divmod(c, 4)
        out[b, ch * T:(ch + 1) * T, :] = np.asarray(res.results[c]["out"]).T
    return out


def kernel(**inputs):
    return run_phases(inputs, full_phases())
```

```python
import numpy as np
import ml_dtypes
import concourse.bass as bass
import concourse.mybir as mybir
from concourse.bass_utils import run_bass_kernel_spmd

F32 = mybir.dt.float32
BF16 = mybir.dt.bfloat16
ALU = mybir.AluOpType
AF = mybir.ActivationFunctionType

NCORES = 8
T = 2048
D = 1024
KC = 8
DFF = 2816
NJ = 22
TB = 512
NTB = T // TB
DEPTH = 4
RMS_EPS = 1e-6
LN_EPS = 1e-5
FF_GROUPS = [6, 6, 5, 5]


class Sem:
    def __init__(self, nc, name):
        self.h = nc.alloc_semaphore(name)
        self.count = 0
        self.name = name


class Buf:
    __slots__ = ("name", "w", "r")

    def __init__(self, name):
        self.name = name
        self.w = None
        self.r = []


class Eng:
    def __init__(self, nc, name, attr):
        self.name = name
        self.attr = attr
        self.sem = Sem(nc, "s_" + name)
        self.ops = []
        self.seen = {}

    def need(self, dep):
        if dep is None:
            return
        sem, val = dep
        if sem is self.sem and (self.name == "pe" or val > sem.count):
            return
        if self.seen.get(sem, 0) >= val:
            return
        self.seen[sem] = val
        h = sem.h
        self.ops.append(lambda e, h=h, val=val: e.wait_ge(h, val))


class Prog:
    def __init__(self, nc):
        self.nc = nc
        self.pe = Eng(nc, "pe", "tensor")
        self.act = Eng(nc, "act", "scalar")
        self.dve = Eng(nc, "dve", "vector")
        self.pool = Eng(nc, "pool", "gpsimd")
        self.sp = Eng(nc, "sp", "sync")
        self.engs = [self.pe, self.act, self.dve, self.pool, self.sp]
        self.dsems = []

    def dma_sem(self, name):
        s = Sem(self.nc, name)
        self.dsems.append(s)
        return s

    def op(self, eng, fn, reads=(), writes=(), inc=True):
        for b in reads:
            eng.need(b.w)
        for b in writes:
            eng.need(b.w)
            for d in b.r:
                eng.need(d)
        val = eng.sem.count + 1
        if inc:
            eng.sem.count = val
            h = eng.sem.h
            eng.ops.append(lambda e, fn=fn, h=h: fn(e).then_inc(h, 1))
        else:
            eng.ops.append(lambda e, fn=fn: fn(e))
        me = (eng.sem, val)
        for b in reads:
            b.r = [d for d in b.r if d[0] is not eng.sem] + [me]
        for b in writes:
            b.w = me
            b.r = []

    def dma(self, eng, fn, dsem, reads=(), writes=(), final=None):
        for b in reads:
            eng.need(b.w)
        for b in writes:
            eng.need(b.w)
            for d in b.r:
                eng.need(d)
        dsem.count += 16
        h = dsem.h
        eng.ops.append(lambda e, fn=fn, h=h: fn(e).then_inc(h, 16))
        me = (dsem, final if final is not None else dsem.count)
        for b in reads:
            b.r = b.r + [me]
        for b in writes:
            b.w = me
            b.r = []

    def coll(self, fn, csem, reads=(), writes=()):
        eng = self.pool
        for b in reads:
            eng.need(b.w)
        for b in writes:
            eng.need(b.w)
            for d in b.r:
                eng.need(d)
        csem.count += 1
        h = csem.h
        eng.ops.append(lambda e, fn=fn, h=h: fn(e).then_inc(h, 1))
        me = (csem, csem.count)
        for b in reads:
            b.r = b.r + [me]
        for b in writes:
            b.w = me
            b.r = []

    def barrier(self, engs=None):
        engs = engs or [self.pe, self.act, self.dve]
        for e in engs:
            for f in engs:
                if f is not e and f.sem.count > 0:
                    e.need((f.sem, f.sem.count))

    def finish(self, final_deps):
        for d in final_deps:
            self.sp.need(d)
        nc = self.nc
        with nc.Block() as block:
            for eng in self.engs:
                def body(e, eng=eng):
                    for f in eng.ops:
                        f(e)
                getattr(block, eng.attr)(body)


class MK:
    def __init__(self, phases, single=False):
        self.phases = phases
        self.single = single
        nc = bass.Bass("TRN2", target_bir_lowering=False)
        self.nc = nc
        self.P = Prog(nc)
        P = self.P
        self.xT_in = nc.dram_tensor("xT", [D, T], F32, kind="ExternalInput").ap()
        self.gains_in = nc.dram_tensor("gains", [128, 13 * KC], F32, kind="ExternalInput").ap()
        self.wgu_in = nc.dram_tensor("wgu", [8 * NJ, 128, KC * 256], F32, kind="ExternalInput").ap()
        self.wdn_in = nc.dram_tensor("wdn", [8 * NJ, 128, D], F32, kind="ExternalInput").ap()
        self.out_ap = nc.dram_tensor("out", [D, T], F32, kind="ExternalOutput").ap()
        self.wcv_in = nc.dram_tensor("wcv", [8, 128, KC * 640], F32, kind="ExternalInput").ap()
        self.wco_in = nc.dram_tensor("wco", [16, 128, D], F32, kind="ExternalInput").ap()
        self.cvp_in = nc.dram_tensor("cvp", [128, 8 * 37], F32, kind="ExternalInput").ap()
        self.wqkv_in = nc.dram_tensor("wqkv", [16, 128, KC * 384], F32, kind="ExternalInput").ap()
        self.wo_in = nc.dram_tensor("wo", [16, 128, D], F32, kind="ExternalInput").ap()
        self.tab_in = nc.dram_tensor("tab", [16, 128, 23 * 128], F32, kind="ExternalInput").ap()
        self.sel_in = nc.dram_tensor("sel", [128, 9], F32, kind="ExternalInput").ap()
        self.xch = {}
        for tag, ncols in (("c0", 32), ("c1", 32), ("a0", T), ("a1", T)):
            xin = nc.dram_tensor("xin_" + tag, [128, KC * ncols], BF16)
            xag = nc.dram_tensor("xag_" + tag, [NCORES * 128, KC * ncols], BF16)
            self.xch[tag] = (xin, xag, Buf("xin" + tag), Buf("xag" + tag))
        self.x = nc.alloc_sbuf_tensor("x", [128, KC, T], F32)
        self.xb = [[Buf(f"x{kc}_{tb}") for tb in range(NTB)] for kc in range(KC)]
        self.hT = nc.alloc_sbuf_tensor("hT", [128, KC, T], BF16)
        self.hb = [[Buf(f"h{kc}_{tb}") for tb in range(NTB)] for kc in range(KC)]
        self.gains = nc.alloc_sbuf_tensor("gains_sb", [128, 13 * KC], F32)
        self.gains_b = Buf("gains")
        self.ones = nc.alloc_sbuf_tensor("ones", [128, 128], BF16)
        self.ones_b = Buf("ones")
        self.eps_rms = nc.alloc_sbuf_tensor("eps_rms", [128, 1], F32)
        self.eps_ln = nc.alloc_sbuf_tensor("eps_ln", [128, 1], F32)
        GMAX = max(FF_GROUPS)
        self.AR = 45056
        self.arena = nc.alloc_sbuf_tensor("arena", [128, self.AR], BF16)
        ar = self.arena
        self.actT = ar[:, 0:GMAX * T].rearrange("p (g t) -> p g t", g=GMAX)
        self.actb = [[Buf(f"a{j}_{tb}") for tb in range(NTB)] for j in range(GMAX)]
        self.NDN = 12
        o = GMAX * T
        self.wdn = ar[:, o:o + self.NDN * D].rearrange("p (s d) -> p s d", s=self.NDN)
        self.wdnb = [Buf(f"wdn{i}") for i in range(self.NDN)]
        self.wdns = [P.dma_sem(f"d_wdn{i}") for i in range(self.NDN)]
        self.wdn_i = 0
        o += self.NDN * D
        self.NGU = 4
        self.wgu = ar[:, o:o + self.NGU * KC * 256].rearrange("p (s d) -> p s d", s=self.NGU)
        self.wgub = [Buf(f"wgu{i}") for i in range(self.NGU)]
        self.wgus = [P.dma_sem(f"d_wgu{i}") for i in range(self.NGU)]
        self.wgu_i = 0
        self.sq = nc.alloc_sbuf_tensor("sq", [128, 1, KC, TB], BF16)
        self.sqb = [[Buf(f"sq{i}_{kc}") for kc in range(KC)] for i in range(1)]
        self.rstd = nc.alloc_sbuf_tensor("rstd", [128, 2, TB], F32)
        self.rstdb = [Buf(f"rstd{i}") for i in range(2)]
        self.sg = nc.alloc_sbuf_tensor("sg", [128, 3, TB], F32)
        self.sgb = [Buf(f"sg{i}") for i in range(3)]
        self.sg_i = 0
        self.cvp = nc.alloc_sbuf_tensor("cvp_sb", [128, 8 * 37], F32)
        self.sel = nc.alloc_sbuf_tensor("sel_sb", [128, 9], F32)
        self.ones32 = nc.alloc_sbuf_tensor("ones32", [128, 128], F32)
        self.vmo = nc.alloc_sbuf_tensor("vmo", [128, 128], BF16)
        self.vmp = nc.alloc_sbuf_tensor("vmp", [128, 128], BF16)
        self.pen = nc.alloc_sbuf_tensor("pen", [128, 2], F32)
        self.pen_b = Buf("pen")
        self.cc_sem = P.dma_sem("cc")
        self.x_sem = P.dma_sem("d_xch")
        self.st_sems = [P.dma_sem("d_st0"), P.dma_sem("d_st1")]
        self.w1_sem = P.dma_sem("d_w1")
        self.w2_sem = P.dma_sem("d_w2")
        self.tab_sem = P.dma_sem("d_tab")
        self.ps = [nc.alloc_psum_tensor(f"ps{i}", [128, TB], F32) for i in range(8)]
        self.psb = [Buf(f"ps{i}") for i in range(8)]
        self.ps_i = 0
        self.misc_sem = P.dma_sem("d_misc")
        self.out_sem = P.dma_sem("d_out")
        self.build()

    def bank(self):
        i = self.ps_i
        self.ps_i = (i + 1) % 8
        return self.ps[i], self.psb[i]

    def load_consts(self):
        P = self.P
        x, xT_in = self.x, self.xT_in
        for kc in range(KC):
            P.dma(P.sp, lambda e, kc=kc: e.dma_start(out=x[:, kc, :], in_=xT_in[kc * 128:(kc + 1) * 128, :]),
                  self.misc_sem, writes=self.xb[kc], final=16 * (KC + 3))
        P.dma(P.sp, lambda e: e.dma_start(out=self.gains[:, :], in_=self.gains_in[:, :]),
              self.misc_sem, writes=[self.gains_b], final=16 * (KC + 3))
        NM = KC + 3
        P.dma(P.sp, lambda e: e.dma_start(out=self.cvp[:, :], in_=self.cvp_in[:, :]),
              self.misc_sem, writes=[Buf("cvp")], final=16 * NM)
        P.dma(P.sp, lambda e: e.dma_start(out=self.sel[:, :], in_=self.sel_in[:, :]),
              self.misc_sem, writes=[Buf("sel")], final=16 * NM)
        P.op(P.dve, lambda e: e.memset(self.ones32[:, :], 1.0 / 512), writes=[self.ones_b], inc=False)
        P.op(P.dve, lambda e: e.memset(self.vmo[:, :], 1.0), writes=[self.ones_b], inc=False)
        P.op(P.dve, lambda e: e.memset(self.ones[:, :], 1.0 / D), writes=[self.ones_b], inc=False)
        P.op(P.dve, lambda e: e.memset(self.eps_rms[:, :], RMS_EPS), writes=[self.ones_b], inc=False)
        P.op(P.dve, lambda e: e.memset(self.eps_ln[:, :], LN_EPS), writes=[self.ones_b])

    def rmsnorm(self, nidx):
        P = self.P
        x, hT = self.x, self.hT
        for tb in range(NTB):
            ts = slice(tb * TB, (tb + 1) * TB)
            si = tb % 2
            for kc in range(KC):
                P.op(P.act, lambda e, kc=kc, ts=ts, si=si: e.activation(
                    out=self.sq[:, 0, kc, :], in_=x[:, kc, ts], func=AF.Square),
                    reads=[self.xb[kc][tb]], writes=[self.sqb[0][kc]])
            ps, psb = self.bank()
            for kc in range(KC):
                P.op(P.pe, lambda e, kc=kc, si=si, ps=ps: e.matmul(
                    ps[:, :], lhsT=self.ones[:, :], rhs=self.sq[:, 0, kc, :], start=(kc == 0), stop=(kc == KC - 1)),
                    reads=[self.ones_b, self.sqb[0][kc]], writes=[psb], inc=(kc == KC - 1))
            P.op(P.act, lambda e, si=si, ps=ps: e.activation(
                out=self.rstd[:, si, :], in_=ps[:, :], func=AF.Sqrt, bias=self.eps_rms[:, :], scale=1.0),
                reads=[psb, self.ones_b], writes=[self.rstdb[si]])
            P.op(P.dve, lambda e, si=si: e.reciprocal(out=self.rstd[:, si, :], in_=self.rstd[:, si, :]),
                reads=[self.rstdb[si]], writes=[self.rstdb[si]])
            for kc in range(KC):
                g = self.gains[:, nidx * KC + kc: nidx * KC + kc + 1]
                P.op(P.dve, lambda e, kc=kc, ts=ts, si=si, g=g: e.scalar_tensor_tensor(
                    out=hT[:, kc, ts], in0=x[:, kc, ts], scalar=g, in1=self.rstd[:, si, :],
                    op0=ALU.mult, op1=ALU.mult),
                    reads=[self.xb[kc][tb], self.rstdb[si], self.gains_b], writes=[self.hb[kc][tb]])

    def ffn(self, fidx, nidx):
        P = self.P
        x, hT, actT = self.x, self.hT, self.actT
        self.rmsnorm(nidx)
        j0 = 0
        for G in FF_GROUPS:
            gu_slots = []
            for jj in range(G):
                s = self.wgu_i
                self.wgu_i = (s + 1) % self.NGU
                gu_slots.append(s)
            dn_slots = []
            for jj in range(G):
                s = self.wdn_i
                self.wdn_i = (s + 1) % self.NDN
                dn_slots.append(s)
            for jj in range(G):
                s = gu_slots[jj]
                src = self.wgu_in[fidx * NJ + j0 + jj]
                P.dma(P.pool, lambda e, s=s, src=src: e.dma_start(out=self.wgu[:, s, :], in_=src),
                      self.wgus[s], writes=[self.wgub[s]])
                for tb in range(NTB):
                    ts = slice(tb * TB, (tb + 1) * TB)
                    pg, pgb = self.bank()
                    pu, pub = self.bank()
                    for half, (pp, ppb) in enumerate(((pg, pgb), (pu, pub))):
                        for kc in range(KC):
                            w = self.wgu[:, s, kc * 256 + half * 128: kc * 256 + half * 128 + 128]
                            P.op(P.pe, lambda e, pp=pp, w=w, kc=kc, ts=ts: e.matmul(
                                pp[:, :], lhsT=w, rhs=hT[:, kc, ts], start=(kc == 0), stop=(kc == KC - 1)),
                                reads=[self.wgub[s], self.hb[kc][tb]], writes=[ppb], inc=(kc == KC - 1))
                    si = self.sg_i
                    self.sg_i = (si + 1) % 3
                    P.op(P.act, lambda e, pg=pg, si=si: e.activation(out=self.sg[:, si, :], in_=pg[:, :], func=AF.Silu),
                         reads=[pgb], writes=[self.sgb[si]])
                    P.op(P.dve, lambda e, pu=pu, si=si, jj=jj, ts=ts: e.tensor_tensor(
                        out=actT[:, jj, ts], in0=self.sg[:, si, :], in1=pu[:, :], op=ALU.mult),
                        reads=[self.sgb[si], pub], writes=[self.actb[jj][tb]])
            for jj in range(G):
                s = dn_slots[jj]
                src = self.wdn_in[fidx * NJ + j0 + jj]
                P.dma(P.pool, lambda e, s=s, src=src: e.dma_start(out=self.wdn[:, s, :], in_=src),
                      self.wdns[s], writes=[self.wdnb[s]])
            for c in range(KC):
                for tb in range(NTB):
                    ts = slice(tb * TB, (tb + 1) * TB)
                    po, pob = self.bank()
                    for jj in range(G):
                        s = dn_slots[jj]
                        P.op(P.pe, lambda e, po=po, s=s, c=c, jj=jj, ts=ts, G=G: e.matmul(
                            po[:, :], lhsT=self.wdn[:, s, c * 128:(c + 1) * 128], rhs=actT[:, jj, ts],
                            start=(jj == 0), stop=(jj == G - 1)),
                            reads=[self.wdnb[s], self.actb[jj][tb]], writes=[pob], inc=(jj == G - 1))
                    P.op(P.dve, lambda e, po=po, c=c, ts=ts: e.scalar_tensor_tensor(
                        out=x[:, c, ts], in0=po[:, :], scalar=0.5, in1=x[:, c, ts], op0=ALU.mult, op1=ALU.add),
                        reads=[pob, self.xb[c][tb]], writes=[self.xb[c][tb]])
            j0 += G

    def exchange(self, tag, ncols, prev, prev_b):
        P = self.P
        xin, xag, xin_b, xag_b = self.xch[tag]
        if self.single:
            P.op(P.dve, lambda e: e.memset(prev, 0.0), writes=[prev_b])
            return
        hb_all = [b for row in self.hb for b in row]
        P.dma(P.sp, lambda e: e.dma_start(out=xin.ap().rearrange("p (k t) -> p k t", k=KC), in_=self.hT[:, :, T - ncols:T]),
              self.x_sem, reads=hb_all, writes=[xin_b])
        P.coll(lambda e: e.collective_compute("AllGather", ALU.bypass, replica_groups=[list(range(NCORES))],
                                              ins=[xin.ap().opt()], outs=[xag.ap().opt()]),
               self.cc_sem, reads=[xin_b], writes=[xag_b])
        i = 0
        for kc in range(KC):
            for r in (0, 1, 2, 4, 5, 6):
                st = self.stage[i % 2]
                stb = self.stage_b[i % 2]
                src = xag.ap()[r * 128:(r + 1) * 128, kc * ncols:(kc + 1) * ncols]
                P.dma(P.sp, lambda e, st=st, src=src: e.dma_start(out=st[:, 0:ncols], in_=src),
                      self.st_sems[i % 2], reads=[xag_b], writes=[stb])
                if r == 0:
                    P.op(P.dve, lambda e, st=st, kc=kc, r=r: e.tensor_scalar(
                        out=prev[:, kc, :], in0=st[:, 0:ncols], scalar1=self.sel[:, r:r + 1], scalar2=None, op0=ALU.mult),
                        reads=[stb, self.gains_b], writes=[prev_b])
                else:
                    P.op(P.dve, lambda e, st=st, kc=kc, r=r: e.scalar_tensor_tensor(
                        out=prev[:, kc, :], in0=st[:, 0:ncols], scalar=self.sel[:, r:r + 1], in1=prev[:, kc, :],
                        op0=ALU.mult, op1=ALU.add),
                        reads=[stb, self.gains_b], writes=[prev_b])
                i += 1

    def proj_residual(self, w, wb, rhs, rhs_b, scale=1.0):
        P = self.P
        x = self.x
        for c in range(KC):
            for tb in range(NTB):
                ts = slice(tb * TB, (tb + 1) * TB)
                po, pob = self.bank()
                P.op(P.pe, lambda e, po=po, c=c, ts=ts: e.matmul(
                    po[:, :], lhsT=w[:, c * 128:(c + 1) * 128], rhs=rhs[:, ts], start=True, stop=True),
                    reads=[wb, rhs_b], writes=[pob])
                P.op(P.dve, lambda e, po=po, c=c, ts=ts: e.scalar_tensor_tensor(
                    out=x[:, c, ts], in0=po[:, :], scalar=scale, in1=x[:, c, ts], op0=ALU.mult, op1=ALU.add),
                    reads=[pob, self.xb[c][tb]], writes=[self.xb[c][tb]])

    def conv(self, ci, nidx):
        P = self.P
        ar = self.arena
        hT = self.hT
        self.rmsnorm(nidx)
        o = 0
        def take(n):
            nonlocal o
            v = ar[:, o:o + n]
            o += n
            return v
        wcv = take(KC * 640).rearrange("p (k n) -> p k n", k=KC); wcv_b = Buf("wcv")
        cx = take(2 * 2056).bitcast(F32); cx_b = Buf("cx")
        u = take(2 * 2080).bitcast(F32); u_b = Buf("u")
        tmp = take(2 * 2 * TB).bitcast(F32).rearrange("p (i t) -> p i t", i=2); tmp_b = [Buf("tmp0"), Buf("tmp1")]
        v = take(2 * 4 * T).bitcast(F32).rearrange("p (q t) -> p q t", q=4); v_b = [[Buf(f"v{q}_{tb}") for tb in range(NTB)] for q in range(4)]
        ya = take(T); ya_b = Buf("ya")
        wout = take(2 * D).rearrange("p (i d) -> p i d", i=2); wout_b = [Buf("wo0"), Buf("wo1")]
        hprev = take(KC * 32).rearrange("p (k t) -> p k t", k=KC); hprev_b = Buf("hprev")
        self.stage = [take(KC * 32), take(KC * 32)]; self.stage_b = [Buf("st0"), Buf("st1")]
        tmp4 = take(2 * 4 * TB).bitcast(F32).rearrange("p (q t) -> p q t", q=4); tmp4_b = [Buf(f"t4{q}") for q in range(4)]
        yb = take(4 * TB).rearrange("p (q t) -> p q t", q=4); yb_b = [Buf(f"yb{q}") for q in range(4)]
        acc = take(2 * TB).bitcast(F32); acc_b = Buf("acc")
        assert o <= self.AR
        self.exchange(f"c{ci}", 32, hprev, hprev_b)
        wo_i = 0
        for q in range(4):
            pc = (ci * 4 + q) * 37
            cp = lambda j, pc=pc: self.cvp[:, pc + j:pc + j + 1]
            P.dma(P.pool, lambda e, q=q: e.dma_start(out=wcv, in_=self.wcv_in[ci * 4 + q].rearrange("p (k n) -> p k n", k=KC)),
                  self.w1_sem, writes=[wcv_b])
            ph, phb = self.bank()
            for g in range(5):
                for kc in range(KC):
                    P.op(P.pe, lambda e, g=g, kc=kc, ph=ph: e.matmul(
                        ph[:, g * 32:(g + 1) * 32], lhsT=wcv[:, kc, g * 128:(g + 1) * 128], rhs=hprev[:, kc, :],
                        start=(kc == 0), stop=(kc == KC - 1)),
                        reads=[wcv_b, hprev_b], writes=[phb], inc=(kc == KC - 1))
            P.op(P.act, lambda e, ph=ph: e.activation(out=tmp[:, 0, 0:32], in_=ph[:, 32:64], func=AF.Copy),
                 reads=[phb], writes=[tmp_b[0]])
            P.op(P.dve, lambda e, ph=ph: e.tensor_tensor(out=cx[:, 0:2], in0=tmp[:, 0, 30:32], in1=ph[:, 94:96], op=ALU.mult),
                 reads=[phb, tmp_b[0]], writes=[cx_b])
            P.op(P.act, lambda e, ph=ph: e.activation(out=tmp[:, 1, 0:32], in_=ph[:, 128:160], func=AF.Sigmoid),
                 reads=[phb], writes=[tmp_b[1]])
            P.op(P.dve, lambda e, ph=ph: e.tensor_tensor(out=u[:, 0:30], in0=tmp[:, 1, 2:32], in1=ph[:, 98:128], op=ALU.mult),
                 reads=[phb, tmp_b[1]], writes=[u_b])
            for tb in range(NTB):
                ts = slice(tb * TB, (tb + 1) * TB)
                pz = []
                for g in range(5):
                    pp, ppb = self.bank()
                    pz.append((pp, ppb))
                    for kc in range(KC):
                        P.op(P.pe, lambda e, g=g, kc=kc, pp=pp, ts=ts: e.matmul(
                            pp[:, :], lhsT=wcv[:, kc, g * 128:(g + 1) * 128], rhs=hT[:, kc, ts],
                            start=(kc == 0), stop=(kc == KC - 1)),
                            reads=[wcv_b, self.hb[kc][tb]], writes=[ppb], inc=(kc == KC - 1))
                (pab, pabb), (pac, pacb), (pax, paxb), (pbv, pbvb), (pbg, pbgb) = pz
                P.op(P.act, lambda e, pac=pac: e.activation(out=tmp[:, 0, :], in_=pac[:, :], func=AF.Copy),
                     reads=[pacb], writes=[tmp_b[0]])
                P.op(P.dve, lambda e, pax=pax, tb=tb: e.tensor_tensor(
                    out=cx[:, 2 + tb * TB:2 + (tb + 1) * TB], in0=tmp[:, 0, :], in1=pax[:, :], op=ALU.mult),
                    reads=[paxb, tmp_b[0]], writes=[cx_b])
                P.op(P.dve, lambda e, tb=tb, cp=cp: e.tensor_scalar(
                    out=acc[:, :], in0=cx[:, tb * TB:tb * TB + TB], scalar1=cp(0), scalar2=None, op0=ALU.mult),
                    reads=[cx_b, self.gains_b], writes=[acc_b])
                for k in (1, 2):
                    P.op(P.dve, lambda e, tb=tb, k=k, cp=cp: e.scalar_tensor_tensor(
                        out=acc[:, :], in0=cx[:, tb * TB + k:tb * TB + k + TB], scalar=cp(k), in1=acc[:, :],
                        op0=ALU.mult, op1=ALU.add),
                        reads=[cx_b, acc_b], writes=[acc_b])
                P.op(P.dve, lambda e, pab=pab, ts=ts: e.tensor_tensor(out=ya[:, ts], in0=acc[:, :], in1=pab[:, :], op=ALU.mult),
                     reads=[acc_b, pabb], writes=[ya_b])
                P.op(P.act, lambda e, pbg=pbg: e.activation(out=tmp[:, 1, :], in_=pbg[:, :], func=AF.Sigmoid),
                     reads=[pbgb], writes=[tmp_b[1]])
                P.op(P.dve, lambda e, pbv=pbv, tb=tb: e.tensor_tensor(
                    out=u[:, 30 + tb * TB:30 + (tb + 1) * TB], in0=tmp[:, 1, :], in1=pbv[:, :], op=ALU.mult),
                    reads=[pbvb, tmp_b[1]], writes=[u_b])
                P.op(P.dve, lambda e, tb=tb, q=q, ts=ts, cp=cp: e.tensor_scalar(
                    out=v[:, q, ts], in0=u[:, tb * TB:tb * TB + TB], scalar1=cp(3), scalar2=cp(34), op0=ALU.mult, op1=ALU.add),
                    reads=[u_b, self.gains_b], writes=[v_b[q][tb]])
                for k in range(1, 31):
                    P.op(P.dve, lambda e, tb=tb, k=k, q=q, ts=ts, cp=cp: e.scalar_tensor_tensor(
                        out=v[:, q, ts], in0=u[:, tb * TB + k:tb * TB + k + TB], scalar=cp(3 + k), in1=v[:, q, ts],
                        op0=ALU.mult, op1=ALU.add),
                        reads=[u_b, v_b[q][tb]], writes=[v_b[q][tb]])
            wi = 0
            wo_i += 1
            P.dma(P.pool, lambda e, wi=wi, q=q: e.dma_start(out=wout[:, wi, :], in_=self.wco_in[ci * 8 + q]),
                  self.w2_sem, writes=[wout_b[wi]])
            self.proj_residual(wout[:, wi, :], wout_b[wi], ya, ya_b)
        wv = ar[:, 0:4 * D].rearrange("p (q d) -> p q d", q=4)
        wv_b = wcv_b
        P.dma(P.pool, lambda e: e.dma_start(out=wv, in_=self.wco_in[ci * 8 + 4:ci * 8 + 8].rearrange("q p d -> p q d")),
              self.w1_sem, writes=[wv_b])
        for tb in range(NTB):
            ts = slice(tb * TB, (tb + 1) * TB)
            pm, pmb = self.bank()
            for q in range(4):
                P.op(P.pe, lambda e, q=q, pm=pm, ts=ts: e.matmul(
                    pm[:, :], lhsT=self.ones32[:, :], rhs=v[:, q, ts], start=(q == 0), stop=(q == 3)),
                    reads=[self.ones_b, v_b[q][tb]], writes=[pmb], inc=(q == 3))
            for q in range(4):
                P.op(P.act, lambda e, q=q, ts=ts: e.activation(out=tmp4[:, q, :], in_=v[:, q, ts], func=AF.Square),
                     reads=[v_b[q][tb]], writes=[tmp4_b[q]])
            pq, pqb = self.bank()
            for q in range(4):
                P.op(P.pe, lambda e, q=q, pq=pq: e.matmul(
                    pq[:, :], lhsT=self.ones32[:, :], rhs=tmp4[:, q, :], start=(q == 0), stop=(q == 3)),
                    reads=[self.ones_b, tmp4_b[q]], writes=[pqb], inc=(q == 3))
            P.op(P.act, lambda e, pm=pm: e.activation(out=tmp[:, 0, :], in_=pm[:, :], func=AF.Square),
                 reads=[pmb], writes=[tmp_b[0]])
            P.op(P.dve, lambda e, pq=pq: e.tensor_tensor(out=tmp[:, 0, :], in0=pq[:, :], in1=tmp[:, 0, :], op=ALU.subtract),
                 reads=[pqb, tmp_b[0]], writes=[tmp_b[0]])
            P.op(P.act, lambda e: e.activation(out=tmp[:, 0, :], in_=tmp[:, 0, :], func=AF.Sqrt, bias=self.eps_ln[:, :], scale=1.0),
                 reads=[tmp_b[0], self.ones_b], writes=[tmp_b[0]])
            P.op(P.dve, lambda e: e.reciprocal(out=tmp[:, 0, :], in_=tmp[:, 0, :]), reads=[tmp_b[0]], writes=[tmp_b[0]])
            for q in range(4):
                pc = (ci * 4 + q) * 37
                P.op(P.dve, lambda e, q=q, ts=ts, pm=pm: e.tensor_tensor(out=tmp4[:, q, :], in0=v[:, q, ts], in1=pm[:, :], op=ALU.subtract),
                     reads=[v_b[q][tb], pmb], writes=[tmp4_b[q]])
                P.op(P.dve, lambda e, q=q: e.tensor_tensor(out=tmp4[:, q, :], in0=tmp4[:, q, :], in1=tmp[:, 0, :], op=ALU.mult),
                     reads=[tmp4_b[q], tmp_b[0]], writes=[tmp4_b[q]])
                P.op(P.act, lambda e, q=q, pc=pc: e.activation(out=yb[:, q, :], in_=tmp4[:, q, :], func=AF.Silu,
                                                            bias=self.cvp[:, pc + 36:pc + 37], scale=self.cvp[:, pc + 35:pc + 36]),
                     reads=[tmp4_b[q], self.gains_b], writes=[yb_b[q]])
            for c in range(KC):
                po, pob = self.bank()
                for q in range(4):
                    P.op(P.pe, lambda e, po=po, c=c, q=q: e.matmul(
                        po[:, :], lhsT=wv[:, q, c * 128:(c + 1) * 128], rhs=yb[:, q, :], start=(q == 0), stop=(q == 3)),
                        reads=[wv_b, yb_b[q]], writes=[pob], inc=(q == 3))
                P.op(P.dve, lambda e, po=po, c=c, ts=ts: e.tensor_tensor(out=self.x[:, c, ts], in0=po[:, :], in1=self.x[:, c, ts], op=ALU.add),
                     reads=[pob, self.xb[c][tb]], writes=[self.xb[c][tb]])

    def attn(self, ai, nidx):
        P = self.P
        ar = self.arena
        hT = self.hT
        self.rmsnorm(nidx)
        o = 0
        def take(n):
            nonlocal o
            v = ar[:, o:o + n]
            o += n
            return v
        hprev = take(KC * T).rearrange("p (k t) -> p k t", k=KC); hprev_b = Buf("hprev")
        Qm = take(T); Qm_b = Buf("Qm")
        Km = take(2 * T); Km_b = Buf("Km")
        Vm = take(32 * 128).rearrange("p (k d) -> p k d", k=32); Vm_b = Buf("Vm")
        tab = take(2 * 23 * 128).bitcast(F32); tab_b = Buf("tab")
        OTm = take(T); OTm_b = Buf("OTm")
        wqkv = take(KC * 384).rearrange("p (k n) -> p k n", k=KC); wqkv_b = Buf("wqkv")
        wo = take(D); wo_b = Buf("wo")
        o_alias = o
        NS = 3
        sT = take(2 * NS * TB).bitcast(F32).rearrange("p (i t) -> p i t", i=NS); sT_b = [Buf(f"sT{i}") for i in range(NS)]
        pT = take(NS * TB).rearrange("p (i t) -> p i t", i=NS); pT_b = [Buf(f"pT{i}") for i in range(NS)]
        rdn = take(2 * TB).bitcast(F32); rdn_b = Buf("rdn")
        assert o - o_alias >= 2 * T
        zt = take(TB); zt_b = Buf("zt")
        assert o <= self.AR, o
        self.stage = [ar[:, o_alias:o_alias + T], ar[:, o_alias + T:o_alias + 2 * T]]; self.stage_b = [Buf("st0"), Buf("st1")]
        self.exchange(f"a{ai}", T, hprev, hprev_b)
        for bb in sT_b + pT_b + [rdn_b]:
            bb.w = hprev_b.w
            bb.r = list(self.stage_b[0].r) + list(self.stage_b[1].r)
        P.op(P.dve, lambda e: e.tensor_scalar(out=self.pen[:, 0:1], in0=self.sel[:, 8:9], scalar1=30000.0, scalar2=-30000.0,
                                               op0=ALU.mult, op1=ALU.add),
             reads=[self.gains_b], writes=[self.pen_b], inc=False)
        P.op(P.dve, lambda e: e.memset(self.pen[:, 1:2], 0.0), writes=[self.pen_b])
        P.op(P.dve, lambda e: e.memset(zt, 0.0), writes=[zt_b])
        SB = [0, 1, 2, 3]
        UB = [4, 5]
        DB = [6, 7]
        for m in range(8):
            P.dma(P.pool, lambda e, m=m: e.dma_start(out=wqkv, in_=self.wqkv_in[ai * 8 + m].rearrange("p (k n) -> p k n", k=KC)),
                  self.w1_sem, writes=[wqkv_b])
            P.dma(P.pool, lambda e, m=m: e.dma_start(out=wo, in_=self.wo_in[ai * 8 + m]), self.w2_sem, writes=[wo_b])
            for tb in range(NTB):
                ts = slice(tb * TB, (tb + 1) * TB)
                pp, ppb = self.bank()
                for kc in range(KC):
                    P.op(P.pe, lambda e, kc=kc, pp=pp, ts=ts: e.matmul(
                        pp[:, :], lhsT=wqkv[:, kc, 0:128], rhs=hT[:, kc, ts], start=(kc == 0), stop=(kc == KC - 1)),
                        reads=[wqkv_b, self.hb[kc][tb]], writes=[ppb], inc=(kc == KC - 1))
                P.op(P.act, lambda e, pp=pp, ts=ts: e.activation(out=Qm[:, ts], in_=pp[:, :], func=AF.Copy),
                     reads=[ppb], writes=[Qm_b])
            for eb in range(8):
                pp, ppb = self.bank()
                for kc in range(KC):
                    if eb < 4:
                        rhs = hprev[:, kc, eb * TB:(eb + 1) * TB]; rb = hprev_b
                    else:
                        rhs = hT[:, kc, (eb - 4) * TB:(eb - 3) * TB]; rb = self.hb[kc][eb - 4]
                    P.op(P.pe, lambda e, kc=kc, pp=pp, rhs=rhs: e.matmul(
                        pp[:, :], lhsT=wqkv[:, kc, 128:256], rhs=rhs, start=(kc == 0), stop=(kc == KC - 1)),
                        reads=[wqkv_b, rb], writes=[ppb], inc=(kc == KC - 1))
                P.op(P.dve, lambda e, pp=pp, eb=eb: e.tensor_copy(out=Km[:, eb * TB:(eb + 1) * TB], in_=pp[:, :]),
                     reads=[ppb], writes=[Km_b])
            for k4 in range(8):
                pp, ppb = self.bank()
                for j in range(4):
                    kb = k4 * 4 + j
                    for kc in range(KC):
                        if kb < 16:
                            lh = hprev[:, kc, kb * 128:(kb + 1) * 128]; rb = hprev_b
                        else:
                            lh = hT[:, kc, (kb - 16) * 128:(kb - 15) * 128]; rb = self.hb[kc][(kb - 16) // 4]
                        P.op(P.pe, lambda e, kc=kc, pp=pp, lh=lh, j=j: e.matmul(
                            pp[:, j * 128:(j + 1) * 128], lhsT=lh, rhs=wqkv[:, kc, 256:384], start=(kc == 0), stop=(kc == KC - 1)),
                            reads=[wqkv_b, rb], writes=[ppb], inc=(kc == KC - 1))
                P.op(P.act, lambda e, pp=pp, k4=k4: e.activation(
                    out=Vm[:, k4 * 4:(k4 + 1) * 4, :], in_=pp[:, :].rearrange("p (j d) -> p j d", j=4), func=AF.Copy),
                    reads=[ppb], writes=[Vm_b])
            steps = [(hh, g, kbe) for hh in range(2) for g in range(4) for kbe in range(4 * g, 4 * g + 20)]
            LOOK = 2
            sbank = {}
            def emit_S(i):
                hh, g, kbe = steps[i]
                hs = slice(hh * 64, (hh + 1) * 64)
                bi = SB[i % 4]
                sbank[i] = bi
                kk = kbe - 4 * g
                c0, c1 = max(0, kk - 16) * 128, (min(3, kk) + 1) * 128
                P.op(P.pe, lambda e, bi=bi, hs=hs, kbe=kbe, g=g, c0=c0, c1=c1: e.matmul(
                    self.ps[bi][:, c0:c1], lhsT=Km[hs, kbe * 128:(kbe + 1) * 128], rhs=Qm[hs, g * TB + c0:g * TB + c1], start=True, stop=True),
                    reads=[Km_b, Qm_b], writes=[self.psb[bi]])
            def emit_rest(i):
                hh, g, kbe = steps[i]
                hs = slice(hh * 64, (hh + 1) * 64)
                bi = sbank[i]
                si = i % NS
                r0 = 4 * g - kbe + 16 + 3
                u = (hh * 4 + g) % 2
                pu, pub = self.ps[UB[u]], self.psb[UB[u]]
                pd, pdb = self.ps[DB[u]], self.psb[DB[u]]
                first, last = (kbe == 4 * g), (kbe == 4 * g + 19)
                if first and g == 0:
                    head = 2 * m + hh
                    P.dma(P.sp, lambda e, head=head: e.dma_start(out=tab, in_=self.tab_in[head]), self.tab_sem, writes=[tab_b])
                kk = kbe - 4 * g
                c0, c1 = max(0, kk - 16) * 128, (min(3, kk) + 1) * 128
                if first:
                    P.op(P.pe, lambda e, pu=pu: e.matmul(pu[:, :], lhsT=self.vmo[:, :], rhs=zt, start=True, stop=False),
                         reads=[self.ones_b, zt_b], writes=[pub], inc=False)
                    P.op(P.pe, lambda e, pd=pd: e.matmul(pd[:, :], lhsT=self.vmo[:, :], rhs=zt, start=True, stop=False),
                         reads=[self.ones_b, zt_b], writes=[pdb])
                P.op(P.dve, lambda e, bi=bi, si=si, r0=r0, c0=c0, c1=c1: e.scalar_tensor_tensor(
                    out=sT[:, si, c0:c1], in0=self.ps[bi][:, c0:c1], scalar=0.125, in1=tab[:, r0 * 128 + c0:r0 * 128 + c1],
                    op0=ALU.mult, op1=ALU.add),
                    reads=[self.psb[bi], tab_b], writes=[sT_b[si]])
                pcol = 0 if kbe < 16 else 1
                P.op(P.act, lambda e, si=si, pcol=pcol, c0=c0, c1=c1: e.activation(out=pT[:, si, c0:c1], in_=sT[:, si, c0:c1], func=AF.Exp,
                                                                                  bias=self.pen[:, pcol:pcol + 1], scale=1.0),
                     reads=[sT_b[si], self.pen_b], writes=[pT_b[si]])
                P.op(P.pe, lambda e, pu=pu, kbe=kbe, si=si, last=last, c0=c0, c1=c1: e.matmul(
                    pu[:, c0:c1], lhsT=Vm[:, kbe, :], rhs=pT[:, si, c0:c1], start=False, stop=last),
                    reads=[Vm_b, pT_b[si]], writes=[pub], inc=False)
                P.op(P.pe, lambda e, pd=pd, si=si, last=last, c0=c0, c1=c1: e.matmul(
                    pd[:, c0:c1], lhsT=self.vmo[:, :], rhs=pT[:, si, c0:c1], start=False, stop=last),
                    reads=[self.ones_b, pT_b[si]], writes=[pdb])
                if last:
                    gs = slice(g * TB, (g + 1) * TB)
                    P.op(P.act, lambda e, pd=pd, hs=hs: e.activation(out=rdn[hs, :], in_=pd[hs, :], func=AF.Copy),
                         reads=[pdb], writes=[rdn_b])
                    P.op(P.dve, lambda e, hs=hs: e.reciprocal(out=rdn[hs, :], in_=rdn[hs, :]), reads=[rdn_b], writes=[rdn_b])
                    P.op(P.dve, lambda e, pu=pu, hs=hs, gs=gs: e.tensor_tensor(out=OTm[hs, gs], in0=pu[hs, :], in1=rdn[hs, :], op=ALU.mult),
                         reads=[pub, rdn_b], writes=[OTm_b])
            for i in range(min(LOOK, len(steps))):
                emit_S(i)
            for i in range(len(steps)):
                if i + LOOK < len(steps):
                    emit_S(i + LOOK)
                emit_rest(i)
            self.proj_residual(wo, wo_b, OTm, OTm_b)

    def final_norm(self):
        P = self.P
        x = self.x
        nidx = 12
        for tb in range(NTB):
            ts = slice(tb * TB, (tb + 1) * TB)
            si = tb % 2
            for kc in range(KC):
                P.op(P.act, lambda e, kc=kc, ts=ts, si=si: e.activation(
                    out=self.sq[:, 0, kc, :], in_=x[:, kc, ts], func=AF.Square),
                    reads=[self.xb[kc][tb]], writes=[self.sqb[0][kc]])
            ps, psb = self.bank()
            for kc in range(KC):
                P.op(P.pe, lambda e, kc=kc, si=si, ps=ps: e.matmul(
                    ps[:, :], lhsT=self.ones[:, :], rhs=self.sq[:, 0, kc, :], start=(kc == 0), stop=(kc == KC - 1)),
                    reads=[self.ones_b, self.sqb[0][kc]], writes=[psb], inc=(kc == KC - 1))
            P.op(P.act, lambda e, si=si, ps=ps: e.activation(
                out=self.rstd[:, si, :], in_=ps[:, :], func=AF.Sqrt, bias=self.eps_rms[:, :], scale=1.0),
                reads=[psb, self.ones_b], writes=[self.rstdb[si]])
            P.op(P.dve, lambda e, si=si: e.reciprocal(out=self.rstd[:, si, :], in_=self.rstd[:, si, :]),
                reads=[self.rstdb[si]], writes=[self.rstdb[si]])
            for kc in range(KC):
                g = self.gains[:, nidx * KC + kc: nidx * KC + kc + 1]
                P.op(P.dve, lambda e, kc=kc, ts=ts, si=si, g=g: e.scalar_tensor_tensor(
                    out=x[:, kc, ts], in0=x[:, kc, ts], scalar=g, in1=self.rstd[:, si, :],
                    op0=ALU.mult, op1=ALU.mult),
                    reads=[self.xb[kc][tb], self.rstdb[si], self.gains_b], writes=[self.xb[kc][tb]])

    def store_x(self):
        P = self.P
        for kc in range(KC):
            P.dma(P.sp, lambda e, kc=kc: e.dma_start(out=self.out_ap[kc * 128:(kc + 1) * 128, :], in_=self.x[:, kc, :]),
                  self.out_sem, reads=self.xb[kc], final=16 * KC)

    def build(self):
        P = self.P
        self.load_consts()
        for ph in self.phases:
            kind = ph[0]
            if kind == "ffn":
                self.ffn(ph[1], ph[2])
            elif kind == "final":
                self.final_norm()
            elif kind == "conv":
                self.conv(ph[1], ph[2])
            elif kind == "attn":
                self.attn(ph[1], ph[2])
            P.barrier(P.engs)
        self.store_x()
        P.finish([(self.out_sem, self.out_sem.count)])


def prep_weights(inp):
    f = lambda k: np.asarray(inp[k], np.float32)
    gains = np.stack([f("ffn1_norm")[l] for l in range(4)] + [f("mix_norm")[l] for l in range(4)]
                     + [f("ffn2_norm")[l] for l in range(4)] + [f("final_norm")], 0)
    gains = np.ascontiguousarray(gains.reshape(13, KC, 128).transpose(2, 0, 1).reshape(128, 13 * KC))
    wgu = np.empty((8, NJ, 128, KC, 256), np.float32)
    wdn = np.empty((8, NJ, 128, D), np.float32)
    for l in range(4):
        for w, (kgu, kdn) in enumerate((("ffn1_w_gate_up", "ffn1_w_down"), ("ffn2_w_gate_up", "ffn2_w_down"))):
            fi = l * 2 + w
            gu = f(kgu)[l]
            g = gu[:, :DFF].reshape(KC, 128, NJ, 128)
            u = gu[:, DFF:].reshape(KC, 128, NJ, 128)
            wgu[fi, :, :, :, :128] = g.transpose(2, 1, 0, 3)
            wgu[fi, :, :, :, 128:] = u.transpose(2, 1, 0, 3)
            wdn[fi] = f(kdn)[l].reshape(NJ, 128, D)
    wcv = np.empty((2, 4, 128, KC, 5, 128), np.float32)
    wco = np.empty((2, 8, 128, D), np.float32)
    cvp = np.empty((128, 2, 4, 37), np.float32)
    for ci in range(2):
        win = f("conv_w_in")[ci].reshape(KC, 128, 5, 4, 128)
        wcv[ci] = win.transpose(3, 1, 0, 2, 4)
        wco[ci] = f("conv_w_out")[ci].reshape(8, 128, D)
        ak = f("conv_a_kernel")[ci].reshape(3, 4, 128)
        bk = f("conv_b_kernel")[ci].reshape(31, 4, 128)
        cvp[:, ci, :, 0:3] = ak.transpose(2, 1, 0)
        cvp[:, ci, :, 3:34] = bk.transpose(2, 1, 0)
        cvp[:, ci, :, 34] = f("conv_b_bias")[ci].reshape(4, 128).T
        cvp[:, ci, :, 35] = f("conv_b_ln_gain")[ci].reshape(4, 128).T
        cvp[:, ci, :, 36] = f("conv_b_ln_bias")[ci].reshape(4, 128).T
    wqkv = np.empty((2, 8, 128, KC, 3, 128), np.float32)
    wo = np.empty((2, 8, 128, D), np.float32)
    for ai in range(2):
        w = f("attn_w_qkv")[ai].reshape(KC, 128, 3, 8, 128)
        wqkv[ai] = w.transpose(3, 1, 0, 2, 4)
        wo[ai] = f("attn_w_o")[ai].reshape(8, 128, D)
    k = np.arange(128)[:, None, None]
    j = np.arange(23)[None, :, None]
    q = np.arange(128)[None, None, :]
    dlt = (j - 3) * 128 + q - k
    mult = ((dlt <= 128).astype(np.float64) + ((dlt % 4 == 0) & (dlt <= 512)) + ((dlt % 16 == 0) & (dlt <= 2048)))
    ok = (dlt >= 0) & (mult > 0)
    lnm = np.log(np.where(ok, mult, 1.0))
    tab = np.empty((16, 128, 23 * 128), np.float32)
    for h in range(16):
        slope = 2.0 ** (-8.0 * (h + 1) / 16)
        tab[h] = np.where(ok, -slope * dlt + lnm, -30000.0).reshape(128, 23 * 128)
    return dict(gains=gains, wgu=wgu.reshape(8 * NJ, 128, KC * 256), wdn=wdn.reshape(8 * NJ, 128, D),
                wcv=wcv.reshape(8, 128, KC * 640), wco=wco.reshape(16, 128, D), cvp=cvp.reshape(128, 8 * 37),
                wqkv=wqkv.reshape(16, 128, KC * 384), wo=wo.reshape(16, 128, D), tab=tab)


def ffn_idx(layer, which):
    return layer * 2 + which


def full_phases():
    ph = []
    for l in range(DEPTH):
        ph.append(("ffn", ffn_idx(l, 0), l))
        if l % 2 == 0:
            ph.append(("conv", l // 2, 4 + l))
        else:
            ph.append(("attn", l // 2, 4 + l))
        ph.append(("ffn", ffn_idx(l, 1), 8 + l))
    ph.append(("final",))
    return ph


def make_in_maps(inp):
    x = np.asarray(inp["x"], np.float32)
    W = prep_weights(inp)
    in_maps = []
    for c in range(NCORES):
        b, ch = divmod(c, 4)
        xT = np.ascontiguousarray(x[b, ch * T:(ch + 1) * T, :].T)
        sel = np.zeros((128, 9), np.float32)
        if ch > 0:
            sel[:, c - 1] = 1.0
            sel[:, 8] = 1.0
        m = dict(W)
        m["xT"] = xT
        m["sel"] = sel
        in_maps.append(m)
    return in_maps


def run_phases(inp, phases):
    x = np.asarray(inp["x"], np.float32)
    mk = MK(phases)
    in_maps = make_in_maps(inp)
    res = run_bass_kernel_spmd(mk.nc, in_maps, core_ids=list(range(NCORES)))
    out = np.empty_like(x)
    for c in range(NCORES):
        b, ch = divmod(c, 4)
        out[b, ch * T:(ch + 1) * T, :] = np.asarray(res.results[c]["out"]).T
    return out


def kernel(**inputs):
    return run_phases(inputs, full_phases())
```
